# Optimizing a Trainium2 kernel written in Bass

```python
import math
import jax, jax.numpy as jnp
from jax import lax
import numpy as np

D_MODEL = 1024
BATCH = 32
SEQ = 2048
DEPTH = 1

SB_HEADS = 16
SB_HEAD_DIM = 64
SB_WIDTH = SB_HEADS * SB_HEAD_DIM
CONV_CHANNELS = D_MODEL
CONV_WIDTH = 31
D_FF = 2816
Q_BLOCK = 128
N_SUBLAYERS = 3
N_MOD = 3
MACARON_WEIGHT = 0.5
DEEPNORM_ALPHA = (2.0 * DEPTH) ** 0.25
DEEPNORM_BETA = (8.0 * DEPTH) ** -0.25
LN_EPS = 1e-5
IN_SPLITS = (SB_WIDTH, SB_WIDTH, SB_WIDTH, CONV_CHANNELS, CONV_CHANNELS, D_MODEL, D_MODEL)
IN_WIDTH = sum(IN_SPLITS)

kernel_name = "hybrid_stickbreak_conformer_macaron_deepnorm_adaln"


def layer_norm(x, g, b):
    xf = x.astype(jnp.float32)
    mu = jnp.mean(xf, axis=-1, keepdims=True)
    var = jnp.mean(jnp.square(xf - mu), axis=-1, keepdims=True)
    y = (xf - mu) * lax.rsqrt(var + LN_EPS) * g.astype(jnp.float32) + b.astype(jnp.float32)
    return y.astype(x.dtype)


def swiglu(u, w_gu, w_down):
    a, g = jnp.split(u @ w_gu, 2, axis=-1)
    return (jax.nn.silu(a) * g) @ w_down


def stick_breaking_attention(q, k, v):
    seq = q.shape[2]
    scale = 1.0 / math.sqrt(q.shape[-1])
    outs = []
    for blk in range(seq // Q_BLOCK):
        start = blk * Q_BLOCK
        end = start + Q_BLOCK
        qb = q[:, :, start:end]
        kb = k[:, :, :end]
        vb = v[:, :, :end]
        z = jnp.einsum('bhqd,bhkd->bhqk', qb, kb).astype(jnp.float32) * scale
        t_idx = start + jnp.arange(Q_BLOCK)[:, None]
        s_idx = jnp.arange(end)[None, :]
        mask = s_idx < t_idx
        log_keep = jnp.where(mask, jax.nn.log_sigmoid(-z), 0.0)
        between = lax.cumsum(log_keep, axis=3, reverse=True) - log_keep
        log_w = jax.nn.log_sigmoid(z) + between
        w = jnp.where(mask, jnp.exp(log_w), 0.0)
        outs.append(jnp.einsum('bhqk,bhkd->bhqd', w.astype(vb.dtype), vb))
    return jnp.concatenate(outs, axis=2)


def conformer_conv(a, b, w_dw, b_dw, g, beta):
    h = a * jax.nn.sigmoid(b)
    h = lax.conv_general_dilated(
        h, w_dw[:, None, :].astype(h.dtype), window_strides=(1,),
        padding=[(CONV_WIDTH - 1, 0)], dimension_numbers=('NWC', 'WIO', 'NWC'),
        feature_group_count=CONV_CHANNELS) + b_dw
    return jax.nn.silu(layer_norm(h, g, beta))


def setup_inputs(seed: int = 0) -> dict:
    key = jax.random.key(seed)
    ks = jax.random.split(key, 32)
    L, D, C = DEPTH, D_MODEL, CONV_CHANNELS

    def nrm(k, shape, std):
        return jax.random.normal(k, shape, jnp.float32) * std

    x = nrm(ks[0], (BATCH, SEQ, D), 1.0)
    c = nrm(ks[1], (BATCH, D), 1.0)
    w_ada = nrm(ks[2], (L, D, N_SUBLAYERS * N_MOD * D), 0.5 * D ** -0.5)
    b_ada = nrm(ks[3], (L, N_SUBLAYERS * N_MOD * D), 0.02)
    ffn1_w_gu = nrm(ks[4], (L, D, 2 * D_FF), D ** -0.5)
    ffn1_w_down = nrm(ks[5], (L, D_FF, D), D_FF ** -0.5 * DEEPNORM_BETA)
    ln1_g = 1.0 + nrm(ks[6], (L, D), 0.02)
    ln1_b = nrm(ks[7], (L, D), 0.02)
    w_qk = nrm(ks[8], (L, D, 2 * SB_WIDTH), D ** -0.5)
    w_v = nrm(ks[9], (L, D, SB_WIDTH), D ** -0.5 * DEEPNORM_BETA)
    w_rest = nrm(ks[10], (L, D, 2 * C + 2 * D), D ** -0.5)
    w_in = jnp.concatenate([w_qk, w_v, w_rest], axis=-1)
    w_sb_out = nrm(ks[11], (L, SB_WIDTH, D), SB_WIDTH ** -0.5 * DEEPNORM_BETA)
    conv_w = nrm(ks[12], (L, CONV_WIDTH, C), CONV_WIDTH ** -0.5)
    conv_b = nrm(ks[13], (L, C), 0.02)
    conv_ln_g = 1.0 + nrm(ks[14], (L, C), 0.02)
    conv_ln_b = nrm(ks[15], (L, C), 0.02)
    w_conv_out = nrm(ks[16], (L, C, D), C ** -0.5 * DEEPNORM_BETA)
    w_out = nrm(ks[17], (L, D, D), D ** -0.5 * DEEPNORM_BETA)
    ln2_g = 1.0 + nrm(ks[18], (L, D), 0.02)
    ln2_b = nrm(ks[19], (L, D), 0.02)
    ffn2_w_gu = nrm(ks[20], (L, D, 2 * D_FF), D ** -0.5)
    ffn2_w_down = nrm(ks[21], (L, D_FF, D), D_FF ** -0.5 * DEEPNORM_BETA)
    ln3_g = 1.0 + nrm(ks[22], (L, D), 0.02)
    ln3_b = nrm(ks[23], (L, D), 0.02)
    return {"x": x, "c": c, "w_ada": w_ada, "b_ada": b_ada,
            "ffn1_w_gu": ffn1_w_gu, "ffn1_w_down": ffn1_w_down, "ln1_g": ln1_g, "ln1_b": ln1_b,
            "w_in": w_in, "w_sb_out": w_sb_out, "conv_w": conv_w, "conv_b": conv_b,
            "conv_ln_g": conv_ln_g, "conv_ln_b": conv_ln_b, "w_conv_out": w_conv_out,
            "w_out": w_out, "ln2_g": ln2_g, "ln2_b": ln2_b,
            "ffn2_w_gu": ffn2_w_gu, "ffn2_w_down": ffn2_w_down, "ln3_g": ln3_g, "ln3_b": ln3_b}


def reference(x, c, w_ada, b_ada, ffn1_w_gu, ffn1_w_down, ln1_g, ln1_b, w_in, w_sb_out,
              conv_w, conv_b, conv_ln_g, conv_ln_b, w_conv_out, w_out, ln2_g, ln2_b,
              ffn2_w_gu, ffn2_w_down, ln3_g, ln3_b):
    bsz, seq, _ = x.shape
    split_idx = list(np.cumsum(IN_SPLITS)[:-1])
    for l in range(DEPTH):
        mod = (jax.nn.silu(c) @ w_ada[l] + b_ada[l]).reshape(bsz, N_SUBLAYERS * N_MOD, 1, D_MODEL)
        sh1, sc1, g1, sh2, sc2, g2, sh3, sc3, g3 = [mod[:, i] for i in range(N_SUBLAYERS * N_MOD)]

        u = x * (1.0 + sc1) + sh1
        x = layer_norm(DEEPNORM_ALPHA * x + g1 * (MACARON_WEIGHT * swiglu(u, ffn1_w_gu[l], ffn1_w_down[l])),
                       ln1_g[l], ln1_b[l])

        u = x * (1.0 + sc2) + sh2
        q, k, v, glu_a, glu_b, gate_a, gate_b = jnp.split(u @ w_in[l], split_idx, axis=-1)
        heads = lambda t: t.reshape(bsz, seq, SB_HEADS, SB_HEAD_DIM).transpose(0, 2, 1, 3)
        y_sb = stick_breaking_attention(heads(q), heads(k), heads(v))
        y_sb = y_sb.transpose(0, 2, 1, 3).reshape(bsz, seq, SB_WIDTH) @ w_sb_out[l]
        y_conv = conformer_conv(glu_a, glu_b, conv_w[l], conv_b[l], conv_ln_g[l], conv_ln_b[l]) @ w_conv_out[l]
        merged = jax.nn.sigmoid(gate_a) * y_sb + jax.nn.sigmoid(gate_b) * y_conv
        x = layer_norm(DEEPNORM_ALPHA * x + g2 * (merged @ w_out[l]), ln2_g[l], ln2_b[l])

        u = x * (1.0 + sc3) + sh3
        x = layer_norm(DEEPNORM_ALPHA * x + g3 * (MACARON_WEIGHT * swiglu(u, ffn2_w_gu[l], ffn2_w_down[l])),
                       ln3_g[l], ln3_b[l])
    return x
```

```python
import contextlib
import numpy as np
import concourse.bass as bass
import concourse.mybir as mybir
from concourse.bass_utils import run_bass_kernel_spmd

F32 = mybir.dt.float32
BF16 = mybir.dt.bfloat16
AF = mybir.ActivationFunctionType
ALU = mybir.AluOpType

D = 1024
NCH = 8
DFF = 2816
NF = 22
NH = 16
NT = 512
CW = 31
ALPHA = 2.0 ** 0.25
LN_EPS = 1e-5
EPS_RES = LN_EPS / (ALPHA * ALPHA)
NCORES = 8

SLOT_ELEMS = 2048
NSLOTS = 6
N_WARM = 2


class Eng:
    def __init__(self, eng, sem, name):
        self.eng = eng
        self.sem = sem
        self.name = name
        self.count = 0
        self.waited = {}


class Tracker:
    def __init__(self, nc, es, needed=None):
        self.nc = nc
        self.needed = needed
        self.rank = None
        if needed is not None:
            self.rank = {name: {v: r + 1 for r, v in enumerate(sorted(vals))} for name, vals in needed.items()}
        self.recorded = {}
        self.sem_owner = {}
        self.last_write = {}
        self.readers = {}
        self.pe = Eng(nc.tensor, es.enter_context(nc.semaphore("sem_pe")), "pe")
        self.act = Eng(nc.scalar, es.enter_context(nc.semaphore("sem_act")), "act")
        self.dve = Eng(nc.vector, es.enter_context(nc.semaphore("sem_dve")), "dve")
        self.pool = Eng(nc.gpsimd, es.enter_context(nc.semaphore("sem_pool")), "pool")
        self.sp = Eng(nc.sync, es.enter_context(nc.semaphore("sem_sp")), "sp")
        self.es = es
        self.dma_sems = {}
        for E in (self.pe, self.act, self.dve, self.pool, self.sp):
            self.sem_owner[id(E.sem)] = E.name
            self.recorded[E.name] = set()

    def _wait(self, E, sem, val):
        key = id(sem)
        if E.waited.get(key, 0) >= val:
            return
        owner = self.sem_owner.get(key)
        if owner is None:
            E.eng.wait_ge(sem, val)
        else:
            self.recorded[owner].add(val)
            E.eng.wait_ge(sem, val if self.rank is None else self.rank[owner][val])
        E.waited[key] = val

    def _collect(self, E, R, W):
        deps = []
        for k in R:
            lw = self.last_write.get(k)
            if lw is not None:
                deps.append((lw, "raw"))
        for k in W:
            lw = self.last_write.get(k)
            if lw is not None:
                deps.append((lw, "waw"))
            for r in self.readers.get(k, ()):
                deps.append((r, "war"))
        need = {}
        for (src, sem, val), kind in deps:
            if src is E:
                if E is self.pe or kind != "raw":
                    continue
            key = id(sem)
            if key not in need or need[key][1] < val:
                need[key] = (sem, val)
        for sem, val in need.values():
            self._wait(E, sem, val)

    def _commit(self, tok, R, W):
        for k in R:
            self.readers.setdefault(k, []).append(tok)
        for k in W:
            self.last_write[k] = tok
            self.readers[k] = []

    def op(self, E, fn, R=(), W=()):
        self._collect(E, R, W)
        ins = fn()
        E.count += 1
        if self.needed is None or E.count in self.needed[E.name]:
            ins.then_inc(E.sem, 1)
        self._commit((E, E.sem, E.count), R, W)

    def dma(self, Q, semname, out, in_, R=(), W=()):
        if semname not in self.dma_sems:
            self.dma_sems[semname] = [self.es.enter_context(self.nc.semaphore("d_" + semname)), 0]
        ent = self.dma_sems[semname]
        self._collect(Q, R, W)
        Q.eng.dma_start(out=out, in_=in_).then_inc(ent[0], 16)
        ent[1] += 16
        self._commit((None, ent[0], ent[1]), R, W)

    def final_wait(self, E):
        for name, (sem, val) in self.dma_sems.items():
            if name.startswith("out"):
                self._wait(E, sem, val)


def build_nc(nseq, S):
    _, needed = _build_nc(nseq, S, None)
    nc, _ = _build_nc(nseq, S, needed)
    return nc


def _build_nc(nseq, S, needed):
    NG = S // NT
    NSB = S // 128
    nc = bass.Bass("TRN2", target_bir_lowering=False)

    def din(name, shape, dt=F32):
        return nc.dram_tensor(name, list(shape), dt, kind="ExternalInput").ap()

    def dscr(name, shape):
        return nc.dram_tensor(name, list(shape), BF16, kind="Internal").ap()

    xT_d = din("xT", [nseq, D, S])
    cT_d = din("cT", [128, NCH, nseq])
    wada_d = din("w_ada_t", [36, 128, 2, NCH, 128])
    bada_d = din("b_ada_t", [128, 72])
    vecs_d = din("vecs", [128, 72 + CW * NCH])
    consts_d = din("consts", [128, 384])
    outT_d = nc.dram_tensor("outT", [nseq, D, S], F32, kind="ExternalOutput").ap()

    wshapes = {
        "gu1": [NF, 128, NCH * 256], "dn1": [2 * NCH, 128, 11 * 128],
        "gu2": [NF, 128, NCH * 256], "dn2": [2 * NCH, 128, 11 * 128],
        "win": [56, 128, NCH * 128], "wv": [4, 128, NCH * 256],
        "wsb": [NCH, 128, NCH * 128], "cdiag": [2 * NCH, 128, 16 * 128], "wco": [NCH, 128, NCH * 128], "wout": [NCH, 128, NCH * 128],
    }
    wf = {k: din("w_" + k, s) for k, s in wshapes.items() if k != "cdiag"}
    wb = {k: dscr("wb_" + k, s) for k, s in wshapes.items()}

    with contextlib.ExitStack() as es:
        T = Tracker(nc, es, needed)
        PE, ACT, DVE, POOL, SP = T.pe, T.act, T.dve, T.pool, T.sp

        def sb(name, shape, dt):
            return es.enter_context(nc.sbuf_tensor(name, list(shape), dt))

        kT = sb("kT", [128, NCH, S], BF16)
        vS = sb("vS", [128, NSB, D], BF16)
        xT = sb("xTt", [128, NCH, NT], F32)
        uT = sb("uT", [128, NCH, NT], BF16)
        big = sb("big", [128, 6144], F32)
        hT = big[:, 0:NF * NT // 2].bitcast(BF16).rearrange("p (f t) -> p f t", f=NF)
        cv = big[:, 0:NCH * NT].rearrange("p (c t) -> p c t", c=NCH)
        mrg = big[:, NCH * NT:NCH * NT + NCH * NT // 2].bitcast(BF16).rearrange("p (c t) -> p c t", c=NCH)
        oT = big[:, 0:NCH * NT].rearrange("p (c t) -> p c t", c=NCH)
        qT = sb("qT", [128, NCH, NT], BF16)
        ysb = sb("ysb", [128, NCH, NT], BF16)
        hn = sb("hn", [128, NCH, NT], BF16)
        hbuf = [sb(f"hbuf{i}", [128, NT + 32], BF16) for i in range(2)]
        halo = sb("halo", [128, NCH, 30], BF16)
        ident = sb("ident", [128, 128], BF16)
        slots = [sb(f"slot{i}", [128, SLOT_ELEMS], BF16) for i in range(NSLOTS)]
        e_t = [sb("e_t0", [128, 2, NT], F32)] * 2
        sp_t = [sb(f"sp_t{i}", [128, 2, NT], BF16) for i in range(3)]
        w_t = [sb(f"w_t{i}", [128, 2, NT], BF16) for i in range(2)]
        spsum = [sb(f"spsum{i}", [128, 2, NT], BF16) for i in range(2)]
        sa_t = [sb(f"sa_t{i}", [128, NT], F32) for i in range(2)]
        ybf = [sb(f"ybf{i}", [128, NT], BF16) for i in range(2)]
        y2bf = [sb(f"y2bf{i}", [128, NT], BF16) for i in range(2)]
        msq = sb("msq", [128, NT], F32)
        rstd = sb("rstd", [128, NT], F32)
        mr = sb("mr", [128, NT], F32)
        tmpf = [sb("tmpf0", [128, NT], F32)] * 2
        vecs = sb("vecs_s", [128, 72 + CW * NCH], F32)
        consts_f = xT[:].rearrange("p c t -> p (c t)")[:, 0:384]
        trineg = sb("trineg", [128, 128], BF16)
        onesneg = sb("onesneg", [128, 128], BF16)
        onesmean = sb("onesmean", [128, 128], BF16)
        mask2 = sb("mask2", [128, 2, 128], BF16)
        cTs = sb("cTs", [128, NCH, nseq], F32)
        siluc = sb("siluc", [128, NCH, nseq], F32)
        bada = sb("bada", [128, 72], F32)
        modT = sb("modT", [128, 72, nseq], F32)
        sc1p = sb("sc1p", [128, 3, NCH, nseq], F32)
        gs = sb("gs", [128, 3, NCH, nseq], F32)
        nA = sb("nA", [128, 2, NCH, nseq], F32)
        nB = sb("nB", [128, 2, NCH, nseq], F32)
        wada_s = [big[:, i * 2048:(i + 1) * 2048].rearrange("p (j k n) -> p j k n", j=2, k=NCH) for i in range(2)]

        psall = es.enter_context(nc.psum_tensor("psall", [128, 8 * NT], F32))
        banks = [psall[:, i * NT:(i + 1) * NT] for i in range(8)]
        ZP = psall[:, 0:2 * NT].rearrange("p (b n) -> p b n", b=2)
        CP = psall[:, 2 * NT:4 * NT].rearrange("p (b n) -> p b n", b=2)
        work_ctr = [0]

        def wbank():
            i = work_ctr[0] % 4
            work_ctr[0] += 1
            return i

        def mm(bi, lhsT, rhs, start, stop, R, rows=128, cols=(0, NT)):
            T.op(PE, lambda: nc.tensor.matmul(banks[bi][0:rows, cols[0]:cols[1]], lhsT=lhsT, rhs=rhs, start=start, stop=stop),
                 R=R, W=[("bank", bi)])

        cast_step = {}
        wb_keys = {}

        def cast_weight(k):
            n0 = wshapes[k][0]
            per = int(np.prod(wshapes[k][1:])) * 4
            step = max(1, (6 << 20) // per)
            cast_step[k] = step
            for a in range(0, n0, step):
                b = min(n0, a + step)
                T.dma(POOL, "cast_" + k, wb[k][a:b], wf[k][a:b], W=[("wb", k, a // step)])
            wb_keys[k] = [("wb", k, p) for p in range((n0 + step - 1) // step)]

        slot_ctr = [0]

        slot_owner = [None] * NSLOTS

        def load_slice(k, idx, parts=128, elems=None):
            if k not in cast_step:
                cast_weight(k)
            si = slot_ctr[0] % NSLOTS
            slot_ctr[0] += 1
            slot_owner[si] = None
            n = wshapes[k][2] if elems is None else elems
            T.dma(SP, f"slot{si}", slots[si][0:parts, 0:n], wb[k][idx], R=wb_keys[k], W=[("slot", si)])
            return si

        def chunk_view(k, idx):
            if k not in cast_step:
                cast_weight(k)
            pidx = idx & ~1
            for si in range(NSLOTS):
                if slot_owner[si] == (k, pidx):
                    break
            else:
                si = slot_ctr[0] % NSLOTS
                slot_ctr[0] += 1
                slot_owner[si] = (k, pidx)
                st = cast_step[k]
                T.dma(SP, f"slot{si}", slots[si][:, 0:2048].rearrange("p (a n) -> p a n", a=2),
                      wb[k][pidx:pidx + 2].rearrange("a p n -> p a n"),
                      R=wb_keys[k], W=[("slot", si)])
            a = idx & 1
            return si, slots[si][:, a * 1024:(a + 1) * 1024].rearrange("p (k n) -> p k n", k=NCH)

        XKEYS = [("xT", c) for c in range(NCH)]
        T.dma(SP, "ld_misc", consts_f, consts_d, W=XKEYS)
        T.dma(SP, "ld_misc", vecs[:], vecs_d, W=["vecs"])
        T.dma(SP, "ld_misc", cTs[:], cT_d, W=["cTs"])
        T.dma(SP, "ld_misc", bada[:], bada_d, W=["bada"])
        for kk in XKEYS + ["vecs", "cTs", "bada"]:
            T.last_write[kk] = T.last_write["bada"]
        cast_weight("gu1")
        cast_weight("dn1")

        T.op(DVE, lambda: nc.vector.tensor_copy(out=trineg[:], in_=consts_f[:, 0:128]), R=XKEYS, W=["trineg"])
        for hh in range(2):
            T.op(DVE, lambda hh=hh: nc.vector.tensor_copy(out=mask2[:, hh, :], in_=consts_f[:, 128:256]), R=XKEYS, W=["mask2"])
        T.op(DVE, lambda: nc.vector.tensor_copy(out=ident[:], in_=consts_f[:, 256:384]), R=XKEYS, W=["ident"])
        T.op(DVE, lambda: nc.vector.memset(onesneg[:], -1.0), W=["onesneg"])
        T.op(DVE, lambda: nc.vector.memset(onesmean[:], 1.0 / D), W=["onesmean"])
        T.op(ACT, lambda: nc.scalar.activation(out=siluc[:], in_=cTs[:], func=AF.Silu), R=["cTs"], W=["siluc"])

        CWOFF = 72
        cast_step["cdiag"] = 1
        wb_keys["cdiag"] = [("wb", "cdiag", p) for p in range(2 * NCH)]
        stg = hn[:].rearrange("p c t -> p (c t)")
        for c in range(NCH):
            for half in range(2):
                sidx = (c * 2 + half) % 2
                sv = stg[:, sidx * 2048:(sidx + 1) * 2048].rearrange("p (j n) -> p j n", j=16)
                for j in range(16 - half):
                    tap = half * 16 + j
                    T.op(DVE, lambda sv=sv, j=j, tap=tap, c=c: nc.vector.tensor_scalar(
                        out=sv[:, j, :], in0=ident[:], scalar1=vecs[:, CWOFF + tap * NCH + c:CWOFF + tap * NCH + c + 1],
                        scalar2=None, op0=ALU.mult), R=["ident", "vecs"], W=[("stg", sidx, j)])
                if half == 1:
                    T.op(DVE, lambda sv=sv: nc.vector.memset(sv[:, 15, :], 0.0), W=[("stg", sidx, 15)])
                T.dma(SP, "st_cdiag%d" % sidx, wb["cdiag"][c * 2 + half], stg[:, sidx * 2048:(sidx + 1) * 2048],
                      R=[("stg", sidx, j) for j in range(16)], W=[("wb", "cdiag", c * 2 + half)])

        for s36 in range(36):
            wbuf = wada_s[s36 % 2]
            T.dma(SP, f"ld_wada{s36 % 2}", wbuf, wada_d[s36], W=[("wada", s36 % 2)])
            for j in range(2):
                m = s36 * 2 + j
                for k in range(NCH):
                    T.op(PE, lambda wbuf=wbuf, j=j, k=k, m=m: nc.tensor.matmul(
                        banks[7][:, m * nseq:(m + 1) * nseq], lhsT=wbuf[:, j, k, :], rhs=siluc[:, k, :],
                        start=(k == 0), stop=(k == NCH - 1)),
                        R=[("wada", s36 % 2), "siluc"], W=[("bank", 7)])
        for m in range(72):
            T.op(DVE, lambda m=m: nc.vector.tensor_scalar(
                out=modT[:, m, :], in0=banks[7][:, m * nseq:(m + 1) * nseq], scalar1=bada[:, m:m + 1], scalar2=None,
                op0=ALU.add), R=[("bank", 7), "bada"], W=["modT"])

        def modv(i, which):
            a = (3 * i + which) * NCH
            return modT[:, a:a + NCH, :]

        LNG = [0, 16, 32]
        LNB = [8, 24, 40]
        CB, CG, CBETA, CWOFF = 48, 56, 64, 72
        for i in range(3):
            T.op(DVE, lambda i=i: nc.vector.tensor_scalar(out=sc1p[:, i], in0=modv(i, 1), scalar1=1.0, scalar2=None, op0=ALU.add),
                 R=["modT"], W=["sc1p"])
            gmul = (0.5 / ALPHA) if i != 1 else (1.0 / ALPHA)
            T.op(DVE, lambda i=i, gmul=gmul: nc.vector.tensor_scalar(out=gs[:, i], in0=modv(i, 2), scalar1=gmul, scalar2=None, op0=ALU.mult),
                 R=["modT"], W=["gs"])
        for i in (1, 2):
            for b in range(nseq):
                T.op(DVE, lambda i=i, b=b: nc.vector.tensor_tensor(
                    out=nA[:, i - 1, :, b], in0=sc1p[:, i, :, b], in1=vecs[:, LNG[i - 1]:LNG[i - 1] + NCH], op=ALU.mult),
                    R=["sc1p", "vecs"], W=["nA"])
                T.op(DVE, lambda i=i, b=b: nc.vector.tensor_tensor(
                    out=nB[:, i - 1, :, b], in0=sc1p[:, i, :, b], in1=vecs[:, LNB[i - 1]:LNB[i - 1] + NCH], op=ALU.mult),
                    R=["sc1p", "vecs"], W=["nB"])
                T.op(DVE, lambda i=i, b=b: nc.vector.tensor_tensor(
                    out=nB[:, i - 1, :, b], in0=nB[:, i - 1, :, b], in1=modT[:, (3 * i) * NCH:(3 * i + 1) * NCH, b], op=ALU.add),
                    R=["nB", "modT"], W=["nB"])
        SETUP_KEYS = ["sc1p", "gs", "nA", "nB", "vecs", "modT"]
        tmp_ctr = [0]

        def epilogue(b, sub, produce, final):
            for c in range(NCH):
                bi = produce(c)
                T.op(DVE, lambda c=c, bi=bi: nc.vector.scalar_tensor_tensor(
                    out=xT[:, c, :], in0=banks[bi][:], scalar=gs[:, sub, c, b:b + 1], in1=xT[:, c, :],
                    op0=ALU.mult, op1=ALU.add), R=[("bank", bi), ("xT", c)] + SETUP_KEYS, W=[("xT", c)])
                ti = tmp_ctr[0] % 2
                tmp_ctr[0] += 1
                T.op(ACT, lambda c=c, ti=ti: nc.scalar.copy(out=ybf[ti][:], in_=xT[:, c, :]), R=[("xT", c)], W=[("ybf", ti)])
                T.op(ACT, lambda c=c, ti=ti: nc.scalar.activation(out=y2bf[ti][:], in_=xT[:, c, :], func=AF.Square),
                     R=[("xT", c)], W=[("y2bf", ti)])
                mm(4, onesmean[:], ybf[ti][:], c == 0, c == NCH - 1, R=["onesmean", ("ybf", ti)])
                mm(5, onesmean[:], y2bf[ti][:], c == 0, c == NCH - 1, R=["onesmean", ("y2bf", ti)])
            ln_finalize(EPS_RES)
            for c in range(NCH):
                normalize(xT[:, c, :], ("xT", c))
                if not final:
                    T.op(ACT, lambda c=c: nc.scalar.activation(
                        out=uT[:, c, :], in_=xT[:, c, :], func=AF.Identity,
                        scale=nA[:, sub, c, b:b + 1], bias=nB[:, sub, c, b:b + 1]),
                        R=[("xT", c)] + SETUP_KEYS, W=[("uT", c)])
                    T.op(ACT, lambda c=c: nc.scalar.activation(
                        out=xT[:, c, :], in_=xT[:, c, :], func=AF.Identity,
                        scale=vecs[:, LNG[sub] + c:LNG[sub] + c + 1], bias=vecs[:, LNB[sub] + c:LNB[sub] + c + 1]),
                        R=[("xT", c)] + SETUP_KEYS, W=[("xT", c)])
                else:
                    T.op(ACT, lambda c=c: nc.scalar.activation(
                        out=oT[:, c, :], in_=xT[:, c, :], func=AF.Identity,
                        scale=vecs[:, LNG[sub] + c:LNG[sub] + c + 1], bias=vecs[:, LNB[sub] + c:LNB[sub] + c + 1]),
                        R=[("xT", c)] + SETUP_KEYS, W=[("oT", c)])

        def ln_finalize(eps):
            T.op(ACT, lambda: nc.scalar.activation(out=msq[:], in_=banks[4][:], func=AF.Square), R=[("bank", 4)], W=["msq"])
            T.op(DVE, lambda: nc.vector.tensor_tensor(out=rstd[:], in0=banks[5][:], in1=msq[:], op=ALU.subtract),
                 R=[("bank", 5), "msq"], W=["rstd"])
            T.op(DVE, lambda: nc.vector.tensor_scalar(out=rstd[:], in0=rstd[:], scalar1=eps, scalar2=None, op0=ALU.add),
                 R=["rstd"], W=["rstd"])
            T.op(ACT, lambda: nc.scalar.activation(out=rstd[:], in_=rstd[:], func=AF.Ln), R=["rstd"], W=["rstd"])
            T.op(ACT, lambda: nc.scalar.activation(out=rstd[:], in_=rstd[:], func=AF.Exp, scale=-0.5), R=["rstd"], W=["rstd"])
            T.op(DVE, lambda: nc.vector.tensor_tensor(out=mr[:], in0=banks[4][:], in1=rstd[:], op=ALU.mult),
                 R=[("bank", 4), "rstd"], W=["mr"])

        def normalize(ap, key):
            T.op(DVE, lambda: nc.vector.tensor_tensor(out=ap, in0=ap, in1=rstd[:], op=ALU.mult), R=[key, "rstd"], W=[key])
            T.op(DVE, lambda: nc.vector.tensor_tensor(out=ap, in0=ap, in1=mr[:], op=ALU.subtract), R=[key, "mr"], W=[key])

        def ffn(b, sub, kgu, kdn, final):
            for f in range(NF):
                si = load_slice(kgu, f)
                wv_ = slots[si][:, 0:NCH * 256].rearrange("p (k n) -> p k n", k=NCH)
                ba = wbank()
                for k in range(NCH):
                    mm(ba, wv_[:, k, 0:128], uT[:, k, :], k == 0, k == NCH - 1, R=[("slot", si), ("uT", k)])
                bg = wbank()
                for k in range(NCH):
                    mm(bg, wv_[:, k, 128:256], uT[:, k, :], k == 0, k == NCH - 1, R=[("slot", si), ("uT", k)])
                ti = tmp_ctr[0] % 2
                tmp_ctr[0] += 1
                T.op(ACT, lambda ba=ba, ti=ti: nc.scalar.activation(out=sa_t[ti][:], in_=banks[ba][:], func=AF.Silu),
                     R=[("bank", ba)], W=[("sa", ti)])
                T.op(DVE, lambda bg=bg, ti=ti, f=f: nc.vector.tensor_tensor(out=hT[:, f, :], in0=banks[bg][:], in1=sa_t[ti][:], op=ALU.mult),
                     R=[("bank", bg), ("sa", ti)], W=[("hT", f)])

            def produce(c):
                bi = wbank()
                for half in range(2):
                    si = load_slice(kdn, c * 2 + half)
                    wd_ = slots[si][:, 0:11 * 128].rearrange("p (f n) -> p f n", f=11)
                    for f2 in range(11):
                        f = half * 11 + f2
                        mm(bi, wd_[:, f2, :], hT[:, f, :], f == 0, f == NF - 1, R=[("slot", si), ("hT", f)])
                return bi
            epilogue(b, sub, produce, final)

        def proj_chunk(j):
            si, w_ = chunk_view("win", j)
            bi = wbank()
            for k in range(NCH):
                mm(bi, w_[:, k, :], uT[:, k, :], k == 0, k == NCH - 1, R=[("slot", si), ("uT", k)])
            return bi

        def mixer(b, g):
            t0 = g * NT
            for j in range(NCH):
                bi = proj_chunk(j)
                T.op(ACT, lambda bi=bi, j=j: nc.scalar.mul(out=qT[:, j, :], in_=banks[bi][:], mul=0.125),
                     R=[("bank", bi)], W=[("qT", j)])
            for j in range(NCH):
                bi = proj_chunk(NCH + j)
                T.op(DVE, lambda bi=bi, j=j: nc.vector.tensor_copy(out=kT[:, j, t0:t0 + NT], in_=banks[bi][:]),
                     R=[("bank", bi)], W=[("kT", j, g)])
            for q4 in range(4):
                si = load_slice("wv", q4)
                w_ = slots[si][:, 0:NCH * 256].rearrange("p (k n) -> p k n", k=NCH)
                for tt in range(4):
                    bi = wbank()
                    for k in range(NCH):
                        T.op(PE, lambda bi=bi, k=k, tt=tt, w_=w_: nc.tensor.matmul(
                            banks[bi][:, 0:256], lhsT=uT[:, k, tt * 128:(tt + 1) * 128], rhs=w_[:, k, :],
                            start=(k == 0), stop=(k == NCH - 1)), R=[("slot", si), ("uT", k)], W=[("bank", bi)])
                    sbk = g * 4 + tt
                    eng = ACT if (tt % 2 == 0) else DVE
                    if eng is ACT:
                        T.op(ACT, lambda bi=bi, sbk=sbk, q4=q4: nc.scalar.copy(out=vS[:, sbk, q4 * 256:(q4 + 1) * 256], in_=banks[bi][:, 0:256]),
                             R=[("bank", bi)], W=[("vS", sbk, q4)])
                    else:
                        T.op(DVE, lambda bi=bi, sbk=sbk, q4=q4: nc.vector.tensor_copy(out=vS[:, sbk, q4 * 256:(q4 + 1) * 256], in_=banks[bi][:, 0:256]),
                             R=[("bank", bi)], W=[("vS", sbk, q4)])
            nblk = 4 * g + 4
            units = [(c, n, i) for c in range(NCH) for n, i in enumerate(range(nblk - 1, -1, -1))]

            def cols_of(i):
                m = i - 4 * g
                return (128 * m if m > 0 else 0), m

            def s1_pe(u, c, n, i):
                lo, m = cols_of(i)
                ss = c % 2
                if n == 0:
                    T.op(POOL, lambda ss=ss: nc.gpsimd.memset(spsum[ss][:], 0.0), W=[("spsum", ss)])
                for hh in range(2):
                    po = hh * 64
                    mm(hh, kT[po:po + 64, c, i * 128:(i + 1) * 128], qT[po:po + 64, c, lo:NT], True, True,
                       R=[("kT", c, i // 4), ("qT", c)], cols=(lo, NT))

            def s1_act(u, c, n, i):
                lo, m = cols_of(i)
                ei, si = 0, u % 3
                T.op(ACT, lambda: nc.scalar.activation(out=e_t[ei][:, :, lo:NT], in_=ZP[:, :, lo:NT], func=AF.Exp),
                     R=[("bank", 0), ("bank", 1)], W=[("e", ei)])
                T.op(ACT, lambda: nc.scalar.activation(out=sp_t[si][:, :, lo:NT], in_=e_t[ei][:, :, lo:NT], func=AF.Ln, bias=1.0),
                     R=[("e", ei)], W=[("sp", si)])
                if m >= 0:
                    T.op(DVE, lambda: nc.vector.tensor_tensor(out=sp_t[si][:, :, lo:lo + 128], in0=sp_t[si][:, :, lo:lo + 128],
                                                              in1=mask2[:], op=ALU.mult), R=[("sp", si), "mask2"], W=[("sp", si)])

            def s2_pe(u, c, n, i):
                lo, m = cols_of(i)
                si = u % 3
                ss = c % 2
                for hh in range(2):
                    po = hh * 64
                    mm(2 + hh, kT[po:po + 64, c, i * 128:(i + 1) * 128], qT[po:po + 64, c, lo:NT], True, False,
                       R=[("kT", c, i // 4), ("qT", c)], cols=(lo, NT))
                    mm(2 + hh, trineg[:], sp_t[si][:, hh, lo:NT], False, n == 0, R=["trineg", ("sp", si)], cols=(lo, NT))
                    if n > 0:
                        mm(2 + hh, onesneg[:], spsum[ss][:, hh, lo:NT], False, True, R=["onesneg", ("spsum", ss)], cols=(lo, NT))
                if i > 0:
                    T.op(DVE, lambda: nc.vector.tensor_tensor(out=spsum[ss][:, :, lo:NT], in0=spsum[ss][:, :, lo:NT],
                                                              in1=sp_t[si][:, :, lo:NT], op=ALU.add),
                         R=[("spsum", ss), ("sp", si)], W=[("spsum", ss)])

            def s2_act(u, c, n, i):
                lo, m = cols_of(i)
                wi = u % 2
                T.op(ACT, lambda: nc.scalar.activation(out=w_t[wi][:, :, lo:NT], in_=CP[:, :, lo:NT], func=AF.Exp),
                     R=[("bank", 2), ("bank", 3)], W=[("w", wi)])
                if m >= 0:
                    T.op(DVE, lambda: nc.vector.tensor_tensor(out=w_t[wi][:, :, lo:lo + 128], in0=w_t[wi][:, :, lo:lo + 128],
                                                              in1=mask2[:], op=ALU.mult), R=[("w", wi), "mask2"], W=[("w", wi)])

            def s3_pe(u, c, n, i):
                lo, m = cols_of(i)
                wi = u % 2
                for hh in range(2):
                    T.op(PE, lambda hh=hh: nc.tensor.matmul(
                        banks[6 + hh][:, lo:NT], lhsT=vS[:, i, c * 128:(c + 1) * 128], rhs=w_t[wi][:, hh, lo:NT],
                        start=(n == 0), stop=(n == nblk - 1), skip_group_check=True),
                        R=[("vS", i, c // 2), ("w", wi)], W=[("bank", 6 + hh)])
                if n == nblk - 1:
                    T.op(DVE, lambda: nc.vector.tensor_copy(out=ysb[0:64, c, :], in_=banks[6][0:64, :]),
                         R=[("bank", 6)], W=[("ysb", c)])
                    T.op(DVE, lambda: nc.vector.tensor_copy(out=ysb[64:128, c, :], in_=banks[7][64:128, :]),
                         R=[("bank", 7)], W=[("ysb", c)])

            def fillers():
                for _ in range(N_WARM):
                    T.op(PE, lambda: nc.tensor.matmul(banks[4][:], lhsT=trineg[:], rhs=ybf[0][:], start=True, stop=True),
                         R=["trineg", ("ybf", 0)], W=[("bank", 4)])

            U = len(units)
            for u0 in range(3):
                s1_pe(u0, *units[u0])
                s1_act(u0, *units[u0])
            s2_pe(0, *units[0])
            s2_act(0, *units[0])
            for u in range(U):
                if u + 1 < U:
                    s2_pe(u + 1, *units[u + 1])
                s3_pe(u, *units[u])
                if u + 3 < U:
                    s1_pe(u + 3, *units[u + 3])
                fillers()
                if u + 1 < U:
                    s2_act(u + 1, *units[u + 1])
                if u + 3 < U:
                    s1_act(u + 3, *units[u + 3])
            for c in range(NCH):
                si, w_ = chunk_view("wsb", c)
                b1 = wbank()
                for k in range(NCH):
                    mm(b1, w_[:, k, :], ysb[:, k, :], k == 0, k == NCH - 1, R=[("slot", si), ("ysb", k)])
                b2 = proj_chunk(40 + c)
                ti = tmp_ctr[0] % 2
                tmp_ctr[0] += 1
                T.op(ACT, lambda b2=b2, ti=ti: nc.scalar.activation(out=sa_t[ti][:], in_=banks[b2][:], func=AF.Sigmoid),
                     R=[("bank", b2)], W=[("sa", ti)])
                T.op(DVE, lambda b1=b1, ti=ti, c=c: nc.vector.tensor_tensor(out=mrg[:, c, :], in0=banks[b1][:], in1=sa_t[ti][:], op=ALU.mult),
                     R=[("bank", b1), ("sa", ti)], W=[("mrg", c)])
            for c in range(NCH):
                b1 = proj_chunk(24 + c)
                b2 = proj_chunk(32 + c)
                ti = tmp_ctr[0] % 2
                tmp_ctr[0] += 1
                hb = hbuf[c % 2]
                T.op(ACT, lambda b2=b2, ti=ti: nc.scalar.activation(out=sa_t[ti][:], in_=banks[b2][:], func=AF.Sigmoid),
                     R=[("bank", b2)], W=[("sa", ti)])
                if g == 0:
                    T.op(POOL, lambda hb=hb: nc.gpsimd.memset(hb[:, 0:30], 0.0), W=[("hbuf", c % 2)])
                else:
                    T.op(POOL, lambda hb=hb, c=c: nc.gpsimd.tensor_copy(out=hb[:, 0:30], in_=halo[:, c, :]),
                         R=[("halo", c)], W=[("hbuf", c % 2)])
                T.op(DVE, lambda b1=b1, ti=ti, hb=hb: nc.vector.tensor_tensor(out=hb[:, 30:30 + NT], in0=banks[b1][:], in1=sa_t[ti][:], op=ALU.mult),
                     R=[("bank", b1), ("sa", ti)], W=[("hbuf", c % 2)])
                T.op(POOL, lambda hb=hb, c=c: nc.gpsimd.tensor_copy(out=halo[:, c, :], in_=hb[:, NT:NT + 30]),
                     R=[("hbuf", c % 2)], W=[("halo", c)])
                bcv = wbank()
                for half in range(2):
                    si = load_slice("cdiag", c * 2 + half)
                    dsl = slots[si][:, 0:2048].rearrange("p (j n) -> p j n", j=16)
                    for j in range(16 - half):
                        tap = half * 16 + j
                        mm(bcv, dsl[:, j, :], hb[:, tap:tap + NT], tap == 0, tap == CW - 1, R=[("slot", si), ("hbuf", c % 2)])
                T.op(DVE, lambda bcv=bcv, c=c: nc.vector.tensor_scalar(
                    out=cv[:, c, :], in0=banks[bcv][:], scalar1=vecs[:, CB + c:CB + c + 1], scalar2=None, op0=ALU.add),
                    R=[("bank", bcv), "vecs"], W=[("cv", c)])
                T.op(ACT, lambda c=c, ti=ti: nc.scalar.copy(out=ybf[ti][:], in_=cv[:, c, :]), R=[("cv", c)], W=[("ybf", ti)])
                T.op(ACT, lambda c=c, ti=ti: nc.scalar.activation(out=y2bf[ti][:], in_=cv[:, c, :], func=AF.Square),
                     R=[("cv", c)], W=[("y2bf", ti)])
                mm(4, onesmean[:], ybf[ti][:], c == 0, c == NCH - 1, R=["onesmean", ("ybf", ti)])
                mm(5, onesmean[:], y2bf[ti][:], c == 0, c == NCH - 1, R=["onesmean", ("y2bf", ti)])
            ln_finalize(LN_EPS)
            for c in range(NCH):
                normalize(cv[:, c, :], ("cv", c))
                T.op(ACT, lambda c=c: nc.scalar.activation(
                    out=hn[:, c, :], in_=cv[:, c, :], func=AF.Silu,
                    scale=vecs[:, CG + c:CG + c + 1], bias=vecs[:, CBETA + c:CBETA + c + 1]),
                    R=[("cv", c), "vecs"], W=[("hn", c)])
            for c in range(NCH):
                si, w_ = chunk_view("wco", c)
                b1 = wbank()
                for k in range(NCH):
                    mm(b1, w_[:, k, :], hn[:, k, :], k == 0, k == NCH - 1, R=[("slot", si), ("hn", k)])
                b2 = proj_chunk(48 + c)
                ti = tmp_ctr[0] % 2
                tmp_ctr[0] += 1
                T.op(ACT, lambda b2=b2, ti=ti: nc.scalar.activation(out=sa_t[ti][:], in_=banks[b2][:], func=AF.Sigmoid),
                     R=[("bank", b2)], W=[("sa", ti)])
                T.op(DVE, lambda b1=b1, ti=ti: nc.vector.tensor_tensor(out=tmpf[ti][:], in0=banks[b1][:], in1=sa_t[ti][:], op=ALU.mult),
                     R=[("bank", b1), ("sa", ti)], W=[("tmpf", 0)])
                T.op(DVE, lambda ti=ti, c=c: nc.vector.tensor_tensor(out=qT[:, c, :], in0=tmpf[ti][:], in1=mrg[:, c, :], op=ALU.add),
                     R=[("tmpf", 0), ("mrg", c)], W=[("qT", c)])

            def produce(c):
                si, w_ = chunk_view("wout", c)
                bi = wbank()
                for k in range(NCH):
                    mm(bi, w_[:, k, :], qT[:, k, :], k == 0, k == NCH - 1, R=[("slot", si), ("qT", k)])
                return bi
            epilogue(b, 1, produce, False)

        for b in range(nseq):
            for g in range(NG):
                t0 = g * NT
                T.dma(POOL, "ld_x", xT[:], xT_d[b].rearrange("(c p) t -> p c t", p=128)[:, :, t0:t0 + NT],
                      W=[("xT", c) for c in range(NCH)])
                for c in range(NCH):
                    T.op(DVE, lambda c=c: nc.vector.tensor_scalar(
                        out=uT[:, c, :], in0=xT[:, c, :], scalar1=sc1p[:, 0, c, b:b + 1], scalar2=modT[:, c, b:b + 1],
                        op0=ALU.mult, op1=ALU.add), R=[("xT", c)] + SETUP_KEYS, W=[("uT", c)])
                ffn(b, 0, "gu1", "dn1", False)
                mixer(b, g)
                ffn(b, 2, "gu2", "dn2", True)
                T.dma(POOL, "out_x", outT_d[b].rearrange("(c p) t -> p c t", p=128)[:, :, t0:t0 + NT], oT,
                      R=[("oT", c) for c in range(NCH)] + [("hT", f) for f in range(NF)])
        T.final_wait(POOL)
    return nc, T.recorded


def _tile_cols(W, width=128):
    K, N = W.shape
    t = W.reshape(K // 128, 128, N // width, width).transpose(2, 1, 0, 3)
    return np.ascontiguousarray(t).reshape(N // width, 128, (K // 128) * width)


def _prep_shared(inp):
    f = lambda a: np.asarray(a, dtype=np.float32)
    sh = {}
    w_ada = f(inp["w_ada"])[0]
    t = w_ada.reshape(NCH, 128, 36, 2, 128).transpose(2, 1, 3, 0, 4)
    sh["w_ada_t"] = np.ascontiguousarray(t)
    sh["b_ada_t"] = np.ascontiguousarray(f(inp["b_ada"])[0].reshape(72, 128).T)

    def fm(v):
        return f(v).reshape(NCH, 128).T

    cols = [fm(inp[k][0]) for k in ("ln1_g", "ln1_b", "ln2_g", "ln2_b", "ln3_g", "ln3_b", "conv_b", "conv_ln_g", "conv_ln_b")]
    cw = f(inp["conv_w"])[0]
    cwt = cw.reshape(CW, NCH, 128).transpose(2, 0, 1).reshape(128, CW * NCH)
    sh["vecs"] = np.ascontiguousarray(np.concatenate(cols + [cwt], axis=1))
    j = np.arange(128)[:, None]
    s = np.arange(128)[None, :]
    trineg = np.where(j >= s, -1.0, 0.0).astype(np.float32)
    mask = np.where(j < s, 1.0, 0.0).astype(np.float32)
    sh["consts"] = np.ascontiguousarray(np.concatenate([trineg, mask, np.eye(128, dtype=np.float32)], axis=1))
    for nm, key in (("gu1", "ffn1_w_gu"), ("gu2", "ffn2_w_gu")):
        W = f(inp[key])[0]
        a = _tile_cols(W[:, :DFF]).reshape(NF, 128, NCH, 128)
        gg = _tile_cols(W[:, DFF:]).reshape(NF, 128, NCH, 128)
        sh["w_" + nm] = np.ascontiguousarray(np.concatenate([a, gg], axis=3)).reshape(NF, 128, NCH * 256)
    for nm, key in (("dn1", "ffn1_w_down"), ("dn2", "ffn2_w_down")):
        t = _tile_cols(f(inp[key])[0]).reshape(NCH, 128, 2, 11 * 128).transpose(0, 2, 1, 3)
        sh["w_" + nm] = np.ascontiguousarray(t).reshape(2 * NCH, 128, 11 * 128)
    w_in = f(inp["w_in"])[0]
    sh["w_win"] = _tile_cols(w_in)
    sh["w_wv"] = _tile_cols(w_in[:, 2048:3072], 256)
    sh["w_wsb"] = _tile_cols(f(inp["w_sb_out"])[0])
    sh["w_wco"] = _tile_cols(f(inp["w_conv_out"])[0])
    sh["w_wout"] = _tile_cols(f(inp["w_out"])[0])
    return sh


def _run(inp, ncores, nseq, S, trace=False):
    sh = _prep_shared(inp)
    x = np.asarray(inp["x"], dtype=np.float32)
    c = np.asarray(inp["c"], dtype=np.float32)
    in_maps = []
    for i in range(ncores):
        xs = x[i * nseq:(i + 1) * nseq]
        cs = c[i * nseq:(i + 1) * nseq]
        m = dict(sh)
        m["xT"] = np.ascontiguousarray(xs.transpose(0, 2, 1))
        m["cT"] = np.ascontiguousarray(cs.reshape(nseq, NCH, 128).transpose(2, 1, 0))
        in_maps.append(m)
    nc = build_nc(nseq, S)
    res = run_bass_kernel_spmd(nc, in_maps, core_ids=list(range(ncores)), **({"trace": True} if trace else {}))
    outs = [np.asarray(r["outT"]).transpose(0, 2, 1) for r in res.results]
    return np.ascontiguousarray(np.concatenate(outs, axis=0)).astype(np.float32), res


def kernel(**inputs):
    B = inputs["x"].shape[0]
    S = inputs["x"].shape[1]
    out, _ = _run(inputs, NCORES, B // NCORES, S)
    return out
```

```python
import contextlib
import numpy as np
import concourse.bass as bass
import concourse.mybir as mybir
from concourse.bass_utils import run_bass_kernel_spmd

F32 = mybir.dt.float32
BF16 = mybir.dt.bfloat16
AF = mybir.ActivationFunctionType
ALU = mybir.AluOpType

D = 1024
NCH = 8
DFF = 2816
NF = 22
NH = 16
NT = 512
CW = 31
ALPHA = 2.0 ** 0.25
LN_EPS = 1e-5
EPS_RES = LN_EPS / (ALPHA * ALPHA)
NCORES = 8
import os
DBG_A = bool(int(os.environ.get('KDBG_A', '0')))

SLOT_ELEMS = 2048
NSLOTS = 6
N_WARM = 2


class Eng:
    def __init__(self, eng, sem, name):
        self.eng = eng
        self.sem = sem
        self.name = name
        self.count = 0
        self.waited = {}


class Tracker:
    def __init__(self, nc, es, needed=None):
        self.nc = nc
        self.needed = needed
        self.rank = None
        if needed is not None:
            self.rank = {name: {v: r + 1 for r, v in enumerate(sorted(vals))} for name, vals in needed.items()}
        self.recorded = {}
        self.sem_owner = {}
        self.last_write = {}
        self.readers = {}
        self.pe = Eng(nc.tensor, es.enter_context(nc.semaphore("sem_pe")), "pe")
        self.act = Eng(nc.scalar, es.enter_context(nc.semaphore("sem_act")), "act")
        self.dve = Eng(nc.vector, es.enter_context(nc.semaphore("sem_dve")), "dve")
        self.pool = Eng(nc.gpsimd, es.enter_context(nc.semaphore("sem_pool")), "pool")
        self.sp = Eng(nc.sync, es.enter_context(nc.semaphore("sem_sp")), "sp")
        self.es = es
        self.dma_sems = {}
        for E in (self.pe, self.act, self.dve, self.pool, self.sp):
            self.sem_owner[id(E.sem)] = E.name
            self.recorded[E.name] = set()

    def _wait(self, E, sem, val):
        key = id(sem)
        if E.waited.get(key, 0) >= val:
            return
        owner = self.sem_owner.get(key)
        if owner is None:
            E.eng.wait_ge(sem, val)
        else:
            self.recorded[owner].add(val)
            E.eng.wait_ge(sem, val if self.rank is None else self.rank[owner][val])
        E.waited[key] = val

    def _collect(self, E, R, W):
        deps = []
        for k in R:
            lw = self.last_write.get(k)
            if lw is not None:
                deps.append((lw, "raw"))
        for k in W:
            lw = self.last_write.get(k)
            if lw is not None:
                deps.append((lw, "waw"))
            for r in self.readers.get(k, ()):
                deps.append((r, "war"))
        need = {}
        for (src, sem, val), kind in deps:
            if src is E:
                if E is self.pe or kind != "raw":
                    continue
            key = id(sem)
            if key not in need or need[key][1] < val:
                need[key] = (sem, val)
        for sem, val in need.values():
            self._wait(E, sem, val)

    def _commit(self, tok, R, W):
        for k in R:
            self.readers.setdefault(k, []).append(tok)
        for k in W:
            self.last_write[k] = tok
            self.readers[k] = []

    def op(self, E, fn, R=(), W=()):
        self._collect(E, R, W)
        ins = fn()
        E.count += 1
        if self.needed is None or E.count in self.needed[E.name]:
            ins.then_inc(E.sem, 1)
        self._commit((E, E.sem, E.count), R, W)

    def dma(self, Q, semname, out, in_, R=(), W=()):
        if semname not in self.dma_sems:
            self.dma_sems[semname] = [self.es.enter_context(self.nc.semaphore("d_" + semname)), 0]
        ent = self.dma_sems[semname]
        self._collect(Q, R, W)
        Q.eng.dma_start(out=out, in_=in_).then_inc(ent[0], 16)
        ent[1] += 16
        self._commit((None, ent[0], ent[1]), R, W)

    def final_wait(self, E):
        for name, (sem, val) in self.dma_sems.items():
            if name.startswith("out"):
                self._wait(E, sem, val)


def build_nc(nseq, S):
    _, needed = _build_nc(nseq, S, None)
    nc, _ = _build_nc(nseq, S, needed)
    return nc


def _build_nc(nseq, S, needed):
    NG = S // NT
    NSB = S // 128
    nc = bass.Bass("TRN2", target_bir_lowering=False)

    def din(name, shape, dt=F32):
        return nc.dram_tensor(name, list(shape), dt, kind="ExternalInput").ap()

    def dscr(name, shape):
        return nc.dram_tensor(name, list(shape), BF16, kind="Internal").ap()

    xT_d = din("xT", [nseq, D, S])
    cT_d = din("cT", [128, NCH, nseq])
    wada_d = din("w_ada_t", [36, 128, 2, NCH, 128])
    bada_d = din("b_ada_t", [128, 72])
    vecs_d = din("vecs", [128, 72 + CW * NCH])
    consts_d = din("consts", [128, 384])
    outT_d = nc.dram_tensor("outT", [nseq, D, S], F32, kind="ExternalOutput").ap()

    wshapes = {
        "gu1": [NF, 128, NCH * 256], "dn1": [2 * NCH, 128, 11 * 128],
        "gu2": [NF, 128, NCH * 256], "dn2": [2 * NCH, 128, 11 * 128],
        "win": [56, 128, NCH * 128], "wv": [4, 128, NCH * 256],
        "wsb": [NCH, 128, NCH * 128], "cdiag": [2 * NCH, 128, 16 * 128], "wco": [NCH, 128, NCH * 128], "wout": [NCH, 128, NCH * 128],
    }
    wf = {k: din("w_" + k, s) for k, s in wshapes.items() if k != "cdiag"}
    wb = {k: dscr("wb_" + k, s) for k, s in wshapes.items()}

    with contextlib.ExitStack() as es:
        T = Tracker(nc, es, needed)
        PE, ACT, DVE, POOL, SP = T.pe, T.act, T.dve, T.pool, T.sp

        def sb(name, shape, dt):
            return es.enter_context(nc.sbuf_tensor(name, list(shape), dt))

        kT = sb("kT", [128, NCH, S], BF16)
        vS = sb("vS", [128, NSB, D], BF16)
        xT = sb("xTt", [128, NCH, NT], F32)
        uT = sb("uT", [128, NCH, NT], BF16)
        big = sb("big", [128, 6144], F32)
        hT = big[:, 0:NF * NT // 2].bitcast(BF16).rearrange("p (f t) -> p f t", f=NF)
        cv = big[:, 0:NCH * NT].rearrange("p (c t) -> p c t", c=NCH)
        mrg = big[:, NCH * NT:NCH * NT + NCH * NT // 2].bitcast(BF16).rearrange("p (c t) -> p c t", c=NCH)
        oT = big[:, 0:NCH * NT].rearrange("p (c t) -> p c t", c=NCH)
        qT = sb("qT", [128, NCH, NT], BF16)
        ysb = sb("ysb", [128, NCH, NT], BF16)
        hn = sb("hn", [128, NCH, NT], BF16)
        hbuf = [sb(f"hbuf{i}", [128, NT + 32], BF16) for i in range(2)]
        halo = sb("halo", [128, NCH, 30], BF16)
        ident = sb("ident", [128, 128], BF16)
        slots = [sb(f"slot{i}", [128, SLOT_ELEMS], BF16) for i in range(NSLOTS)]
        e_t = [sb("e_t0", [128, 2, NT], F32)] * 2
        sp_t = [sb(f"sp_t{i}", [128, 2, NT], BF16) for i in range(3)]
        w_t = [sb(f"w_t{i}", [128, 2, NT], BF16) for i in range(2)]
        spsum = [sb(f"spsum{i}", [128, 2, NT], BF16) for i in range(2)]
        sa_t = [sb(f"sa_t{i}", [128, NT], F32) for i in range(2)]
        ybf = [sb(f"ybf{i}", [128, NT], BF16) for i in range(2)]
        y2bf = [sb(f"y2bf{i}", [128, NT], BF16) for i in range(2)]
        msq = sb("msq", [128, NT], F32)
        rstd = sb("rstd", [128, NT], F32)
        mr = sb("mr", [128, NT], F32)
        tmpf = [sb("tmpf0", [128, NT], F32)] * 2
        vecs = sb("vecs_s", [128, 72 + CW * NCH], F32)
        consts_f = xT[:].rearrange("p c t -> p (c t)")[:, 0:384]
        trineg = sb("trineg", [128, 128], BF16)
        onesneg = sb("onesneg", [128, 128], BF16)
        onesmean = sb("onesmean", [128, 128], BF16)
        mask2 = sb("mask2", [128, 2, 128], BF16)
        cTs = sb("cTs", [128, NCH, nseq], F32)
        siluc = sb("siluc", [128, NCH, nseq], F32)
        bada = sb("bada", [128, 72], F32)
        modT = sb("modT", [128, 72, nseq], F32)
        sc1p = sb("sc1p", [128, 3, NCH, nseq], F32)
        gs = sb("gs", [128, 3, NCH, nseq], F32)
        nA = sb("nA", [128, 2, NCH, nseq], F32)
        nB = sb("nB", [128, 2, NCH, nseq], F32)
        wada_s = []
        for tns, nel in ((kT, NCH * S), (vS, NSB * D)):
            flat = tns[:].rearrange("p a b -> p (a b)").bitcast(F32)
            for i in range(min(4, (nel // 2) // 2048)):
                wada_s.append(flat[:, i * 2048:(i + 1) * 2048].rearrange("p (j k n) -> p j k n", j=2, k=NCH))
        NWB = len(wada_s)

        psall = es.enter_context(nc.psum_tensor("psall", [128, 8 * NT], F32))
        banks = [psall[:, i * NT:(i + 1) * NT] for i in range(8)]
        ZP = psall[:, 0:2 * NT].rearrange("p (b n) -> p b n", b=2)
        CP = psall[:, 2 * NT:4 * NT].rearrange("p (b n) -> p b n", b=2)
        work_ctr = [0]

        def wbank():
            i = work_ctr[0] % 4
            work_ctr[0] += 1
            return i

        def mm(bi, lhsT, rhs, start, stop, R, rows=128, cols=(0, NT)):
            T.op(PE, lambda: nc.tensor.matmul(banks[bi][0:rows, cols[0]:cols[1]], lhsT=lhsT, rhs=rhs, start=start, stop=stop),
                 R=R, W=[("bank", bi)])

        cast_step = {}
        wb_keys = {}

        def cast_weight(k, after=()):
            n0 = wshapes[k][0]
            per = int(np.prod(wshapes[k][1:])) * 4
            step = max(1, (6 << 20) // per)
            cast_step[k] = step
            for a in range(0, n0, step):
                b = min(n0, a + step)
                T.dma(POOL, "cast_" + k, wb[k][a:b], wf[k][a:b], R=list(after), W=[("wb", k, a // step)])
            wb_keys[k] = [("wb", k, p) for p in range((n0 + step - 1) // step)]

        slot_ctr = [0]

        slot_owner = [None] * NSLOTS

        def load_slice(k, idx, parts=128, elems=None):
            if k not in cast_step:
                cast_weight(k)
            si = slot_ctr[0] % NSLOTS
            slot_ctr[0] += 1
            slot_owner[si] = None
            n = wshapes[k][2] if elems is None else elems
            T.dma(SP, f"slot{si}", slots[si][0:parts, 0:n], wb[k][idx], R=wb_keys[k], W=[("slot", si)])
            return si

        def chunk_view(k, idx):
            if k not in cast_step:
                cast_weight(k)
            pidx = idx & ~1
            for si in range(NSLOTS):
                if slot_owner[si] == (k, pidx):
                    break
            else:
                si = slot_ctr[0] % NSLOTS
                slot_ctr[0] += 1
                slot_owner[si] = (k, pidx)
                st = cast_step[k]
                T.dma(SP, f"slot{si}", slots[si][:, 0:2048].rearrange("p (a n) -> p a n", a=2),
                      wb[k][pidx:pidx + 2].rearrange("a p n -> p a n"),
                      R=wb_keys[k], W=[("slot", si)])
            a = idx & 1
            return si, slots[si][:, a * 1024:(a + 1) * 1024].rearrange("p (k n) -> p k n", k=NCH)

        XKEYS = [("xT", c) for c in range(NCH)]
        T.dma(SP, "ld_misc", consts_f, consts_d, W=XKEYS)
        T.dma(SP, "ld_misc", vecs[:], vecs_d, W=["vecs"])
        T.dma(SP, "ld_misc", cTs[:], cT_d, W=["cTs"])
        T.dma(SP, "ld_misc", bada[:], bada_d, W=["bada"])
        for kk in XKEYS + ["vecs", "cTs", "bada"]:
            T.last_write[kk] = T.last_write["bada"]

        cast_weight("gu1")
        cast_weight("dn1")
        T.op(DVE, lambda: nc.vector.tensor_copy(out=trineg[:], in_=consts_f[:, 0:128]), R=XKEYS, W=["trineg"])
        for hh in range(2):
            T.op(DVE, lambda hh=hh: nc.vector.tensor_copy(out=mask2[:, hh, :], in_=consts_f[:, 128:256]), R=XKEYS, W=["mask2"])
        T.op(DVE, lambda: nc.vector.tensor_copy(out=ident[:], in_=consts_f[:, 256:384]), R=XKEYS, W=["ident"])
        T.op(DVE, lambda: nc.vector.memset(onesneg[:], -1.0), W=["onesneg"])
        T.op(DVE, lambda: nc.vector.memset(onesmean[:], 1.0 / D), W=["onesmean"])
        T.op(ACT, lambda: nc.scalar.activation(out=siluc[:], in_=cTs[:], func=AF.Silu), R=["cTs"], W=["siluc"])

        CWOFF = 72
        cast_step["cdiag"] = 1
        wb_keys["cdiag"] = [("wb", "cdiag", p) for p in range(2 * NCH)]
        stg = hn[:].rearrange("p c t -> p (c t)")
        for c in range(NCH):
            for half in range(2):
                sidx = (c * 2 + half) % 2
                sv = stg[:, sidx * 2048:(sidx + 1) * 2048].rearrange("p (j n) -> p j n", j=16)
                for j in range(16 - half):
                    tap = half * 16 + j
                    T.op(DVE, lambda sv=sv, j=j, tap=tap, c=c: nc.vector.tensor_scalar(
                        out=sv[:, j, :], in0=ident[:], scalar1=vecs[:, CWOFF + tap * NCH + c:CWOFF + tap * NCH + c + 1],
                        scalar2=None, op0=ALU.mult), R=["ident", "vecs"], W=[("stg", sidx, j)])
                if half == 1:
                    T.op(DVE, lambda sv=sv: nc.vector.memset(sv[:, 15, :], 0.0), W=[("stg", sidx, 15)])
                T.dma(POOL, "st_cdiag%d" % sidx, wb["cdiag"][c * 2 + half], stg[:, sidx * 2048:(sidx + 1) * 2048],
                      R=[("stg", sidx, j) for j in range(16)], W=[("wb", "cdiag", c * 2 + half)])

        for s36 in range(36):
            wbuf = wada_s[s36 % NWB]
            T.dma(SP, f"ld_wada{s36 % NWB}", wbuf, wada_d[s36], W=[("wada", s36 % NWB)])
            for j in range(2):
                m = s36 * 2 + j
                for k in range(NCH):
                    T.op(PE, lambda wbuf=wbuf, j=j, k=k, m=m: nc.tensor.matmul(
                        banks[7][:, m * nseq:(m + 1) * nseq], lhsT=wbuf[:, j, k, :], rhs=siluc[:, k, :],
                        start=(k == 0), stop=(k == NCH - 1)),
                        R=[("wada", s36 % NWB), "siluc"], W=[("bank", 7)])
        for m in range(72):
            T.op(DVE, lambda m=m: nc.vector.tensor_scalar(
                out=modT[:, m, :], in0=banks[7][:, m * nseq:(m + 1) * nseq], scalar1=bada[:, m:m + 1], scalar2=None,
                op0=ALU.add), R=[("bank", 7), "bada"], W=["modT"])

        def modv(i, which):
            a = (3 * i + which) * NCH
            return modT[:, a:a + NCH, :]

        LNG = [0, 16, 32]
        LNB = [8, 24, 40]
        CB, CG, CBETA, CWOFF = 48, 56, 64, 72
        for i in range(3):
            T.op(DVE, lambda i=i: nc.vector.tensor_scalar(out=sc1p[:, i], in0=modv(i, 1), scalar1=1.0, scalar2=None, op0=ALU.add),
                 R=["modT"], W=["sc1p"])
            gmul = (0.5 / ALPHA) if i != 1 else (1.0 / ALPHA)
            T.op(DVE, lambda i=i, gmul=gmul: nc.vector.tensor_scalar(out=gs[:, i], in0=modv(i, 2), scalar1=gmul, scalar2=None, op0=ALU.mult),
                 R=["modT"], W=["gs"])
        for i in (1, 2):
            for b in range(nseq):
                T.op(DVE, lambda i=i, b=b: nc.vector.tensor_tensor(
                    out=nA[:, i - 1, :, b], in0=sc1p[:, i, :, b], in1=vecs[:, LNG[i - 1]:LNG[i - 1] + NCH], op=ALU.mult),
                    R=["sc1p", "vecs"], W=["nA"])
                T.op(DVE, lambda i=i, b=b: nc.vector.tensor_tensor(
                    out=nB[:, i - 1, :, b], in0=sc1p[:, i, :, b], in1=vecs[:, LNB[i - 1]:LNB[i - 1] + NCH], op=ALU.mult),
                    R=["sc1p", "vecs"], W=["nB"])
                T.op(DVE, lambda i=i, b=b: nc.vector.tensor_tensor(
                    out=nB[:, i - 1, :, b], in0=nB[:, i - 1, :, b], in1=modT[:, (3 * i) * NCH:(3 * i + 1) * NCH, b], op=ALU.add),
                    R=["nB", "modT"], W=["nB"])
        SETUP_KEYS = ["sc1p", "gs", "nA", "nB", "vecs", "modT"]
        tmp_ctr = [0]

        def epilogue(b, sub, produce, final):
            pend = None
            for c in range(NCH):
                bi = produce(c)
                if pend is not None and not DBG_A:
                    stats_mm(*pend)
                T.op(DVE, lambda c=c, bi=bi: nc.vector.scalar_tensor_tensor(
                    out=xT[:, c, :], in0=banks[bi][:], scalar=gs[:, sub, c, b:b + 1], in1=xT[:, c, :],
                    op0=ALU.mult, op1=ALU.add), R=[("bank", bi), ("xT", c)] + SETUP_KEYS, W=[("xT", c)])
                ti = tmp_ctr[0] % 2
                tmp_ctr[0] += 1
                T.op(ACT, lambda c=c, ti=ti: nc.scalar.copy(out=ybf[ti][:], in_=xT[:, c, :]), R=[("xT", c)], W=[("ybf", ti)])
                T.op(ACT, lambda c=c, ti=ti: nc.scalar.activation(out=y2bf[ti][:], in_=xT[:, c, :], func=AF.Square),
                     R=[("xT", c)], W=[("y2bf", ti)])
                pend = (c, ti)
                if DBG_A:
                    stats_mm(*pend)
            if not DBG_A:
                stats_mm(*pend)
            ln_finalize(EPS_RES)
            for c in range(NCH):
                normalize(xT[:, c, :], ("xT", c))
                if not final:
                    T.op(ACT, lambda c=c: nc.scalar.activation(
                        out=uT[:, c, :], in_=xT[:, c, :], func=AF.Identity,
                        scale=nA[:, sub, c, b:b + 1], bias=nB[:, sub, c, b:b + 1]),
                        R=[("xT", c)] + SETUP_KEYS, W=[("uT", c)])
                    T.op(ACT, lambda c=c: nc.scalar.activation(
                        out=xT[:, c, :], in_=xT[:, c, :], func=AF.Identity,
                        scale=vecs[:, LNG[sub] + c:LNG[sub] + c + 1], bias=vecs[:, LNB[sub] + c:LNB[sub] + c + 1]),
                        R=[("xT", c)] + SETUP_KEYS, W=[("xT", c)])
                else:
                    T.op(ACT, lambda c=c: nc.scalar.activation(
                        out=oT[:, c, :], in_=xT[:, c, :], func=AF.Identity,
                        scale=vecs[:, LNG[sub] + c:LNG[sub] + c + 1], bias=vecs[:, LNB[sub] + c:LNB[sub] + c + 1]),
                        R=[("xT", c)] + SETUP_KEYS, W=[("oT", c)])

        def stats_mm(c, ti):
            mm(4, onesmean[:], ybf[ti][:], c == 0, c == NCH - 1, R=["onesmean", ("ybf", ti)])
            mm(5, onesmean[:], y2bf[ti][:], c == 0, c == NCH - 1, R=["onesmean", ("y2bf", ti)])

        def ln_finalize(eps):
            T.op(ACT, lambda: nc.scalar.activation(out=msq[:], in_=banks[4][:], func=AF.Square), R=[("bank", 4)], W=["msq"])
            T.op(DVE, lambda: nc.vector.tensor_tensor(out=rstd[:], in0=banks[5][:], in1=msq[:], op=ALU.subtract),
                 R=[("bank", 5), "msq"], W=["rstd"])
            T.op(DVE, lambda: nc.vector.tensor_scalar(out=rstd[:], in0=rstd[:], scalar1=eps, scalar2=None, op0=ALU.add),
                 R=["rstd"], W=["rstd"])
            T.op(ACT, lambda: nc.scalar.activation(out=rstd[:], in_=rstd[:], func=AF.Ln), R=["rstd"], W=["rstd"])
            T.op(ACT, lambda: nc.scalar.activation(out=rstd[:], in_=rstd[:], func=AF.Exp, scale=-0.5), R=["rstd"], W=["rstd"])
            T.op(DVE, lambda: nc.vector.tensor_tensor(out=mr[:], in0=banks[4][:], in1=rstd[:], op=ALU.mult),
                 R=[("bank", 4), "rstd"], W=["mr"])

        def normalize(ap, key):
            T.op(DVE, lambda: nc.vector.tensor_tensor(out=ap, in0=ap, in1=rstd[:], op=ALU.mult), R=[key, "rstd"], W=[key])
            T.op(DVE, lambda: nc.vector.tensor_tensor(out=ap, in0=ap, in1=mr[:], op=ALU.subtract), R=[key, "mr"], W=[key])

        def ffn(b, sub, kgu, kdn, final):
            for f in range(NF):
                si = load_slice(kgu, f)
                wv_ = slots[si][:, 0:NCH * 256].rearrange("p (k n) -> p k n", k=NCH)
                ba = wbank()
                for k in range(NCH):
                    mm(ba, wv_[:, k, 0:128], uT[:, k, :], k == 0, k == NCH - 1, R=[("slot", si), ("uT", k)])
                bg = wbank()
                for k in range(NCH):
                    mm(bg, wv_[:, k, 128:256], uT[:, k, :], k == 0, k == NCH - 1, R=[("slot", si), ("uT", k)])
                ti = tmp_ctr[0] % 2
                tmp_ctr[0] += 1
                T.op(ACT, lambda ba=ba, ti=ti: nc.scalar.activation(out=sa_t[ti][:], in_=banks[ba][:], func=AF.Silu),
                     R=[("bank", ba)], W=[("sa", ti)])
                T.op(DVE, lambda bg=bg, ti=ti, f=f: nc.vector.tensor_tensor(out=hT[:, f, :], in0=banks[bg][:], in1=sa_t[ti][:], op=ALU.mult),
                     R=[("bank", bg), ("sa", ti)], W=[("hT", f)])

            def produce(c):
                bi = wbank()
                for half in range(2):
                    si = load_slice(kdn, c * 2 + half)
                    wd_ = slots[si][:, 0:11 * 128].rearrange("p (f n) -> p f n", f=11)
                    for f2 in range(11):
                        f = half * 11 + f2
                        mm(bi, wd_[:, f2, :], hT[:, f, :], f == 0, f == NF - 1, R=[("slot", si), ("hT", f)])
                return bi
            epilogue(b, sub, produce, final)

        def proj_chunk(j):
            si, w_ = chunk_view("win", j)
            bi = wbank()
            for k in range(NCH):
                mm(bi, w_[:, k, :], uT[:, k, :], k == 0, k == NCH - 1, R=[("slot", si), ("uT", k)])
            return bi

        def mixer(b, g):
            t0 = g * NT
            for j in range(NCH):
                bi = proj_chunk(j)
                T.op(ACT, lambda bi=bi, j=j: nc.scalar.mul(out=qT[:, j, :], in_=banks[bi][:], mul=0.125),
                     R=[("bank", bi)], W=[("qT", j)])
            for j in range(NCH):
                bi = proj_chunk(NCH + j)
                T.op(DVE, lambda bi=bi, j=j: nc.vector.tensor_copy(out=kT[:, j, t0:t0 + NT], in_=banks[bi][:]),
                     R=[("bank", bi)], W=[("kT", j, g)])
            for q4 in range(4):
                si = load_slice("wv", q4)
                w_ = slots[si][:, 0:NCH * 256].rearrange("p (k n) -> p k n", k=NCH)
                for tt in range(4):
                    bi = wbank()
                    for k in range(NCH):
                        T.op(PE, lambda bi=bi, k=k, tt=tt, w_=w_: nc.tensor.matmul(
                            banks[bi][:, 0:256], lhsT=uT[:, k, tt * 128:(tt + 1) * 128], rhs=w_[:, k, :],
                            start=(k == 0), stop=(k == NCH - 1)), R=[("slot", si), ("uT", k)], W=[("bank", bi)])
                    sbk = g * 4 + tt
                    eng = ACT if (tt % 2 == 0) else DVE
                    if eng is ACT:
                        T.op(ACT, lambda bi=bi, sbk=sbk, q4=q4: nc.scalar.copy(out=vS[:, sbk, q4 * 256:(q4 + 1) * 256], in_=banks[bi][:, 0:256]),
                             R=[("bank", bi)], W=[("vS", sbk, q4)])
                    else:
                        T.op(DVE, lambda bi=bi, sbk=sbk, q4=q4: nc.vector.tensor_copy(out=vS[:, sbk, q4 * 256:(q4 + 1) * 256], in_=banks[bi][:, 0:256]),
                             R=[("bank", bi)], W=[("vS", sbk, q4)])
            if b == 0 and g == 0:
                for kk in ("wsb", "wco", "wout", "gu2", "dn2"):
                    cast_weight(kk)
            nblk = 4 * g + 4
            units = [(c, n, i) for c in range(NCH) for n, i in enumerate(range(nblk - 1, -1, -1))]

            def cols_of(i):
                m = i - 4 * g
                return (128 * m if m > 0 else 0), m

            def s1_pe(u, c, n, i):
                lo, m = cols_of(i)
                ss = c % 2
                if n == 0:
                    T.op(POOL, lambda ss=ss: nc.gpsimd.memset(spsum[ss][:], 0.0), W=[("spsum", ss)])
                for hh in range(2):
                    po = hh * 64
                    mm(hh, kT[po:po + 64, c, i * 128:(i + 1) * 128], qT[po:po + 64, c, lo:NT], True, True,
                       R=[("kT", c, i // 4), ("qT", c)], cols=(lo, NT))

            def s1_act(u, c, n, i):
                lo, m = cols_of(i)
                ei, si = 0, u % 3
                T.op(ACT, lambda: nc.scalar.activation(out=e_t[ei][:, :, lo:NT], in_=ZP[:, :, lo:NT], func=AF.Exp),
                     R=[("bank", 0), ("bank", 1)], W=[("e", ei)])
                T.op(ACT, lambda: nc.scalar.activation(out=sp_t[si][:, :, lo:NT], in_=e_t[ei][:, :, lo:NT], func=AF.Ln, bias=1.0),
                     R=[("e", ei)], W=[("sp", si)])
                if m >= 0:
                    T.op(DVE, lambda: nc.vector.tensor_tensor(out=sp_t[si][:, :, lo:lo + 128], in0=sp_t[si][:, :, lo:lo + 128],
                                                              in1=mask2[:], op=ALU.mult), R=[("sp", si), "mask2"], W=[("sp", si)])

            def s2_pe(u, c, n, i):
                lo, m = cols_of(i)
                si = u % 3
                ss = c % 2
                for hh in range(2):
                    po = hh * 64
                    mm(2 + hh, kT[po:po + 64, c, i * 128:(i + 1) * 128], qT[po:po + 64, c, lo:NT], True, False,
                       R=[("kT", c, i // 4), ("qT", c)], cols=(lo, NT))
                    mm(2 + hh, trineg[:], sp_t[si][:, hh, lo:NT], False, n == 0, R=["trineg", ("sp", si)], cols=(lo, NT))
                    if n > 0:
                        mm(2 + hh, onesneg[:], spsum[ss][:, hh, lo:NT], False, True, R=["onesneg", ("spsum", ss)], cols=(lo, NT))
                if i > 0:
                    T.op(DVE, lambda: nc.vector.tensor_tensor(out=spsum[ss][:, :, lo:NT], in0=spsum[ss][:, :, lo:NT],
                                                              in1=sp_t[si][:, :, lo:NT], op=ALU.add),
                         R=[("spsum", ss), ("sp", si)], W=[("spsum", ss)])

            def s2_act(u, c, n, i):
                lo, m = cols_of(i)
                wi = u % 2
                T.op(ACT, lambda: nc.scalar.activation(out=w_t[wi][:, :, lo:NT], in_=CP[:, :, lo:NT], func=AF.Exp),
                     R=[("bank", 2), ("bank", 3)], W=[("w", wi)])
                if m >= 0:
                    T.op(DVE, lambda: nc.vector.tensor_tensor(out=w_t[wi][:, :, lo:lo + 128], in0=w_t[wi][:, :, lo:lo + 128],
                                                              in1=mask2[:], op=ALU.mult), R=[("w", wi), "mask2"], W=[("w", wi)])

            def s3_pe(u, c, n, i):
                lo, m = cols_of(i)
                wi = u % 2
                for hh in range(2):
                    T.op(PE, lambda hh=hh: nc.tensor.matmul(
                        banks[6 + hh][:, lo:NT], lhsT=vS[:, i, c * 128:(c + 1) * 128], rhs=w_t[wi][:, hh, lo:NT],
                        start=(n == 0), stop=(n == nblk - 1), skip_group_check=True),
                        R=[("vS", i, c // 2), ("w", wi)], W=[("bank", 6 + hh)])
                if n == nblk - 1:
                    T.op(DVE, lambda: nc.vector.tensor_copy(out=ysb[0:64, c, :], in_=banks[6][0:64, :]),
                         R=[("bank", 6)], W=[("ysb", c)])
                    T.op(DVE, lambda: nc.vector.tensor_copy(out=ysb[64:128, c, :], in_=banks[7][64:128, :]),
                         R=[("bank", 7)], W=[("ysb", c)])

            def fillers():
                for _ in range(N_WARM):
                    T.op(PE, lambda: nc.tensor.matmul(banks[4][:], lhsT=trineg[:], rhs=ybf[0][:], start=True, stop=True),
                         R=["trineg", ("ybf", 0)], W=[("bank", 4)])

            U = len(units)
            for u0 in range(3):
                s1_pe(u0, *units[u0])
                s1_act(u0, *units[u0])
            s2_pe(0, *units[0])
            s2_act(0, *units[0])
            for u in range(U):
                if u + 1 < U:
                    s2_pe(u + 1, *units[u + 1])
                s3_pe(u, *units[u])
                if u + 3 < U:
                    s1_pe(u + 3, *units[u + 3])
                fillers()
                if u + 1 < U:
                    s2_act(u + 1, *units[u + 1])
                if u + 3 < U:
                    s1_act(u + 3, *units[u + 3])
            def conv_P(c):
                b1 = proj_chunk(24 + c)
                b2 = proj_chunk(32 + c)
                ti = tmp_ctr[0] % 2
                tmp_ctr[0] += 1
                hb = hbuf[c % 2]
                T.op(ACT, lambda: nc.scalar.activation(out=sa_t[ti][:], in_=banks[b2][:], func=AF.Sigmoid),
                     R=[("bank", b2)], W=[("sa", ti)])
                if g == 0:
                    T.op(POOL, lambda: nc.gpsimd.memset(hb[:, 0:30], 0.0), W=[("hbuf", c % 2)])
                else:
                    T.op(POOL, lambda: nc.gpsimd.tensor_copy(out=hb[:, 0:30], in_=halo[:, c, :]),
                         R=[("halo", c)], W=[("hbuf", c % 2)])
                T.op(DVE, lambda: nc.vector.tensor_tensor(out=hb[:, 30:30 + NT], in0=banks[b1][:], in1=sa_t[ti][:], op=ALU.mult),
                     R=[("bank", b1), ("sa", ti)], W=[("hbuf", c % 2)])
                T.op(POOL, lambda: nc.gpsimd.tensor_copy(out=halo[:, c, :], in_=hb[:, NT:NT + 30]),
                     R=[("hbuf", c % 2)], W=[("halo", c)])

            def conv_C(c):
                hb = hbuf[c % 2]
                ti = c % 2
                bcv = wbank()
                for half in range(2):
                    si = load_slice("cdiag", c * 2 + half)
                    dsl = slots[si][:, 0:2048].rearrange("p (j n) -> p j n", j=16)
                    for j in range(16 - half):
                        tap = half * 16 + j
                        mm(bcv, dsl[:, j, :], hb[:, tap:tap + NT], tap == 0, tap == CW - 1, R=[("slot", si), ("hbuf", c % 2)])
                T.op(DVE, lambda: nc.vector.tensor_scalar(
                    out=cv[:, c, :], in0=banks[bcv][:], scalar1=vecs[:, CB + c:CB + c + 1], scalar2=None, op0=ALU.add),
                    R=[("bank", bcv), "vecs"], W=[("cv", c)])
                T.op(ACT, lambda: nc.scalar.copy(out=ybf[ti][:], in_=cv[:, c, :]), R=[("cv", c)], W=[("ybf", ti)])
                T.op(ACT, lambda: nc.scalar.activation(out=y2bf[ti][:], in_=cv[:, c, :], func=AF.Square),
                     R=[("cv", c)], W=[("y2bf", ti)])
                return (c, ti)

            conv_P(0)
            pend = None
            for c in range(NCH):
                if c + 1 < NCH:
                    conv_P(c + 1)
                nxt = conv_C(c)
                if pend is not None:
                    stats_mm(*pend)
                pend = nxt
            stats_mm(*pend)
            ln_finalize(LN_EPS)
            for c in range(NCH):
                si, w_ = chunk_view("wsb", c)
                b1 = wbank()
                for k in range(NCH):
                    mm(b1, w_[:, k, :], ysb[:, k, :], k == 0, k == NCH - 1, R=[("slot", si), ("ysb", k)])
                b2 = proj_chunk(40 + c)
                ti = tmp_ctr[0] % 2
                tmp_ctr[0] += 1
                T.op(ACT, lambda b2=b2, ti=ti: nc.scalar.activation(out=sa_t[ti][:], in_=banks[b2][:], func=AF.Sigmoid),
                     R=[("bank", b2)], W=[("sa", ti)])
                normalize(cv[:, c, :], ("cv", c))
                T.op(DVE, lambda b1=b1, ti=ti, c=c: nc.vector.tensor_tensor(out=mrg[:, c, :], in0=banks[b1][:], in1=sa_t[ti][:], op=ALU.mult),
                     R=[("bank", b1), ("sa", ti)], W=[("mrg", c)])
            for c in range(NCH):
                T.op(ACT, lambda c=c: nc.scalar.activation(
                    out=hn[:, c, :], in_=cv[:, c, :], func=AF.Silu,
                    scale=vecs[:, CG + c:CG + c + 1], bias=vecs[:, CBETA + c:CBETA + c + 1]),
                    R=[("cv", c), "vecs"], W=[("hn", c)])
            for c in range(NCH):
                si, w_ = chunk_view("wco", c)
                b1 = wbank()
                for k in range(NCH):
                    mm(b1, w_[:, k, :], hn[:, k, :], k == 0, k == NCH - 1, R=[("slot", si), ("hn", k)])
                b2 = proj_chunk(48 + c)
                ti = tmp_ctr[0] % 2
                tmp_ctr[0] += 1
                T.op(ACT, lambda b2=b2, ti=ti: nc.scalar.activation(out=sa_t[ti][:], in_=banks[b2][:], func=AF.Sigmoid),
                     R=[("bank", b2)], W=[("sa", ti)])
                T.op(DVE, lambda b1=b1, ti=ti: nc.vector.tensor_tensor(out=tmpf[ti][:], in0=banks[b1][:], in1=sa_t[ti][:], op=ALU.mult),
                     R=[("bank", b1), ("sa", ti)], W=[("tmpf", 0)])
                T.op(DVE, lambda ti=ti, c=c: nc.vector.tensor_tensor(out=qT[:, c, :], in0=tmpf[ti][:], in1=mrg[:, c, :], op=ALU.add),
                     R=[("tmpf", 0), ("mrg", c)], W=[("qT", c)])

            def produce(c):
                si, w_ = chunk_view("wout", c)
                bi = wbank()
                for k in range(NCH):
                    mm(bi, w_[:, k, :], qT[:, k, :], k == 0, k == NCH - 1, R=[("slot", si), ("qT", k)])
                return bi
            epilogue(b, 1, produce, False)

        for b in range(nseq):
            for g in range(NG):
                t0 = g * NT
                T.dma(POOL, "ld_x", xT[:], xT_d[b].rearrange("(c p) t -> p c t", p=128)[:, :, t0:t0 + NT],
                      W=[("xT", c) for c in range(NCH)])
                if b == 0 and g == 0:
                    cast_weight("win")
                for c in range(NCH):
                    T.op(DVE, lambda c=c: nc.vector.tensor_scalar(
                        out=uT[:, c, :], in0=xT[:, c, :], scalar1=sc1p[:, 0, c, b:b + 1], scalar2=modT[:, c, b:b + 1],
                        op0=ALU.mult, op1=ALU.add), R=[("xT", c)] + SETUP_KEYS, W=[("uT", c)])
                ffn(b, 0, "gu1", "dn1", False)
                mixer(b, g)
                ffn(b, 2, "gu2", "dn2", True)
                T.dma(POOL, "out_x", outT_d[b].rearrange("(c p) t -> p c t", p=128)[:, :, t0:t0 + NT], oT,
                      R=[("oT", c) for c in range(NCH)] + [("hT", f) for f in range(NF)])
        T.final_wait(POOL)
    return nc, T.recorded


def _tile_cols(W, width=128):
    K, N = W.shape
    t = W.reshape(K // 128, 128, N // width, width).transpose(2, 1, 0, 3)
    return np.ascontiguousarray(t).reshape(N // width, 128, (K // 128) * width)


def _prep_shared(inp):
    f = lambda a: np.asarray(a, dtype=np.float32)
    sh = {}
    w_ada = f(inp["w_ada"])[0]
    t = w_ada.reshape(NCH, 128, 36, 2, 128).transpose(2, 1, 3, 0, 4)
    sh["w_ada_t"] = np.ascontiguousarray(t)
    sh["b_ada_t"] = np.ascontiguousarray(f(inp["b_ada"])[0].reshape(72, 128).T)

    def fm(v):
        return f(v).reshape(NCH, 128).T

    cols = [fm(inp[k][0]) for k in ("ln1_g", "ln1_b", "ln2_g", "ln2_b", "ln3_g", "ln3_b", "conv_b", "conv_ln_g", "conv_ln_b")]
    cw = f(inp["conv_w"])[0]
    cwt = cw.reshape(CW, NCH, 128).transpose(2, 0, 1).reshape(128, CW * NCH)
    sh["vecs"] = np.ascontiguousarray(np.concatenate(cols + [cwt], axis=1))
    j = np.arange(128)[:, None]
    s = np.arange(128)[None, :]
    trineg = np.where(j >= s, -1.0, 0.0).astype(np.float32)
    mask = np.where(j < s, 1.0, 0.0).astype(np.float32)
    sh["consts"] = np.ascontiguousarray(np.concatenate([trineg, mask, np.eye(128, dtype=np.float32)], axis=1))
    for nm, key in (("gu1", "ffn1_w_gu"), ("gu2", "ffn2_w_gu")):
        W = f(inp[key])[0]
        a = _tile_cols(W[:, :DFF]).reshape(NF, 128, NCH, 128)
        gg = _tile_cols(W[:, DFF:]).reshape(NF, 128, NCH, 128)
        sh["w_" + nm] = np.ascontiguousarray(np.concatenate([a, gg], axis=3)).reshape(NF, 128, NCH * 256)
    for nm, key in (("dn1", "ffn1_w_down"), ("dn2", "ffn2_w_down")):
        t = _tile_cols(f(inp[key])[0]).reshape(NCH, 128, 2, 11 * 128).transpose(0, 2, 1, 3)
        sh["w_" + nm] = np.ascontiguousarray(t).reshape(2 * NCH, 128, 11 * 128)
    w_in = f(inp["w_in"])[0]
    sh["w_win"] = _tile_cols(w_in)
    sh["w_wv"] = _tile_cols(w_in[:, 2048:3072], 256)
    sh["w_wsb"] = _tile_cols(f(inp["w_sb_out"])[0])
    sh["w_wco"] = _tile_cols(f(inp["w_conv_out"])[0])
    sh["w_wout"] = _tile_cols(f(inp["w_out"])[0])
    return sh


def _run(inp, ncores, nseq, S, trace=False):
    sh = _prep_shared(inp)
    x = np.asarray(inp["x"], dtype=np.float32)
    c = np.asarray(inp["c"], dtype=np.float32)
    in_maps = []
    for i in range(ncores):
        xs = x[i * nseq:(i + 1) * nseq]
        cs = c[i * nseq:(i + 1) * nseq]
        m = dict(sh)
        m["xT"] = np.ascontiguousarray(xs.transpose(0, 2, 1))
        m["cT"] = np.ascontiguousarray(cs.reshape(nseq, NCH, 128).transpose(2, 1, 0))
        in_maps.append(m)
    nc = build_nc(nseq, S)
    res = run_bass_kernel_spmd(nc, in_maps, core_ids=list(range(ncores)), **({"trace": True} if trace else {}))
    outs = [np.asarray(r["outT"]).transpose(0, 2, 1) for r in res.results]
    return np.ascontiguousarray(np.concatenate(outs, axis=0)).astype(np.float32), res


def kernel(**inputs):
    B = inputs["x"].shape[0]
    S = inputs["x"].shape[1]
    out, _ = _run(inputs, NCORES, B // NCORES, S)
    return out
```

```python
import contextlib
import numpy as np
import concourse.bass as bass
import concourse.mybir as mybir
from concourse.bass_utils import run_bass_kernel_spmd

F32 = mybir.dt.float32
BF16 = mybir.dt.bfloat16
AF = mybir.ActivationFunctionType
ALU = mybir.AluOpType

D = 1024
NCH = 8
DFF = 2816
NF = 22
NH = 16
NT = 512
CW = 31
ALPHA = 2.0 ** 0.25
LN_EPS = 1e-5
EPS_RES = LN_EPS / (ALPHA * ALPHA)
NCORES = 8
import os
DBG_A = bool(int(os.environ.get('KDBG_A', '0')))

SLOT_ELEMS = 2048
NSLOTS = 6
N_WARM = 2


class Eng:
    def __init__(self, eng, sem, name):
        self.eng = eng
        self.sem = sem
        self.name = name
        self.count = 0
        self.waited = {}


class Tracker:
    def __init__(self, nc, es, needed=None):
        self.nc = nc
        self.needed = needed
        self.rank = None
        if needed is not None:
            self.rank = {name: {v: r + 1 for r, v in enumerate(sorted(vals))} for name, vals in needed.items()}
        self.recorded = {}
        self.sem_owner = {}
        self.last_write = {}
        self.readers = {}
        self.pe = Eng(nc.tensor, es.enter_context(nc.semaphore("sem_pe")), "pe")
        self.act = Eng(nc.scalar, es.enter_context(nc.semaphore("sem_act")), "act")
        self.dve = Eng(nc.vector, es.enter_context(nc.semaphore("sem_dve")), "dve")
        self.pool = Eng(nc.gpsimd, es.enter_context(nc.semaphore("sem_pool")), "pool")
        self.sp = Eng(nc.sync, es.enter_context(nc.semaphore("sem_sp")), "sp")
        self.es = es
        self.dma_sems = {}
        for E in (self.pe, self.act, self.dve, self.pool, self.sp):
            self.sem_owner[id(E.sem)] = E.name
            self.recorded[E.name] = set()

    def _wait(self, E, sem, val):
        key = id(sem)
        if E.waited.get(key, 0) >= val:
            return
        owner = self.sem_owner.get(key)
        if owner is None:
            E.eng.wait_ge(sem, val)
        else:
            self.recorded[owner].add(val)
            E.eng.wait_ge(sem, val if self.rank is None else self.rank[owner][val])
        E.waited[key] = val

    def _collect(self, E, R, W):
        deps = []
        for k in R:
            lw = self.last_write.get(k)
            if lw is not None:
                deps.append((lw, "raw"))
        for k in W:
            lw = self.last_write.get(k)
            if lw is not None:
                deps.append((lw, "waw"))
            for r in self.readers.get(k, ()):
                deps.append((r, "war"))
        need = {}
        for (src, sem, val), kind in deps:
            if src is E:
                if E is self.pe or kind != "raw":
                    continue
            key = id(sem)
            if key not in need or need[key][1] < val:
                need[key] = (sem, val)
        for sem, val in need.values():
            self._wait(E, sem, val)

    def _commit(self, tok, R, W):
        for k in R:
            self.readers.setdefault(k, []).append(tok)
        for k in W:
            self.last_write[k] = tok
            self.readers[k] = []

    def op(self, E, fn, R=(), W=()):
        self._collect(E, R, W)
        ins = fn()
        E.count += 1
        if self.needed is None or E.count in self.needed[E.name]:
            ins.then_inc(E.sem, 1)
        self._commit((E, E.sem, E.count), R, W)

    def dma(self, Q, semname, out, in_, R=(), W=()):
        if semname not in self.dma_sems:
            self.dma_sems[semname] = [self.es.enter_context(self.nc.semaphore("d_" + semname)), 0]
        ent = self.dma_sems[semname]
        self._collect(Q, R, W)
        Q.eng.dma_start(out=out, in_=in_).then_inc(ent[0], 16)
        ent[1] += 16
        self._commit((None, ent[0], ent[1]), R, W)

    def final_wait(self, E):
        for name, (sem, val) in self.dma_sems.items():
            if name.startswith("out"):
                self._wait(E, sem, val)


def build_nc(nseq, S):
    _, needed = _build_nc(nseq, S, None)
    nc, _ = _build_nc(nseq, S, needed)
    return nc


def _build_nc(nseq, S, needed):
    NG = S // NT
    NSB = S // 128
    nc = bass.Bass("TRN2", target_bir_lowering=False)

    def din(name, shape, dt=F32):
        return nc.dram_tensor(name, list(shape), dt, kind="ExternalInput").ap()

    def dscr(name, shape):
        return nc.dram_tensor(name, list(shape), BF16, kind="Internal").ap()

    xT_d = din("xT", [nseq, D, S])
    cT_d = din("cT", [128, NCH, nseq])
    wada_d = din("w_ada_t", [36, 128, 2, NCH, 128])
    bada_d = din("b_ada_t", [128, 72])
    vecs_d = din("vecs", [128, 72 + CW * NCH])
    consts_d = din("consts", [128, 384])
    outT_d = nc.dram_tensor("outT", [nseq, D, S], F32, kind="ExternalOutput").ap()

    wshapes = {
        "gu1": [NF, 128, NCH * 256], "dn1": [2 * NCH, 128, 11 * 128],
        "gu2": [NF, 128, NCH * 256], "dn2": [2 * NCH, 128, 11 * 128],
        "win": [56, 128, NCH * 128], "wv": [4, 128, NCH * 256],
        "wsb": [NCH, 128, NCH * 128], "cdiag": [2 * NCH, 128, 16 * 128], "wco": [NCH, 128, NCH * 128], "wout": [NCH, 128, NCH * 128],
    }
    wf = {k: din("w_" + k, s) for k, s in wshapes.items() if k != "cdiag"}
    wb = {k: dscr("wb_" + k, s) for k, s in wshapes.items()}

    with contextlib.ExitStack() as es:
        T = Tracker(nc, es, needed)
        PE, ACT, DVE, POOL, SP = T.pe, T.act, T.dve, T.pool, T.sp

        def sb(name, shape, dt):
            return es.enter_context(nc.sbuf_tensor(name, list(shape), dt))

        kT = sb("kT", [128, NCH, S], BF16)
        vS = sb("vS", [128, NSB, D], BF16)
        xT = sb("xTt", [128, NCH, NT], F32)
        uT = sb("uT", [128, NCH, NT], BF16)
        big = sb("big", [128, 6144], F32)
        hT = big[:, 0:NF * NT // 2].bitcast(BF16).rearrange("p (f t) -> p f t", f=NF)
        cv = big[:, 0:NCH * NT].rearrange("p (c t) -> p c t", c=NCH)
        mrg = big[:, NCH * NT:NCH * NT + NCH * NT // 2].bitcast(BF16).rearrange("p (c t) -> p c t", c=NCH)
        oT = big[:, 0:NCH * NT].rearrange("p (c t) -> p c t", c=NCH)
        qT = sb("qT", [128, NCH, NT], BF16)
        ysb = sb("ysb", [128, NCH, NT], BF16)
        hn = sb("hn", [128, NCH, NT], BF16)
        hbuf = [sb(f"hbuf{i}", [128, NT + 32], BF16) for i in range(2)]
        halo = sb("halo", [128, NCH, 30], BF16)
        ident = sb("ident", [128, 128], BF16)
        slots = [sb(f"slot{i}", [128, SLOT_ELEMS], BF16) for i in range(NSLOTS)]
        e_t = [sb("e_t0", [128, 2, NT], F32)] * 2
        sp_t = [sb(f"sp_t{i}", [128, 2, NT], BF16) for i in range(3)]
        w_t = [sb(f"w_t{i}", [128, 2, NT], BF16) for i in range(2)]
        spsum = [sb(f"spsum{i}", [128, 2, NT], BF16) for i in range(2)]
        sa_t = [sb(f"sa_t{i}", [128, NT], F32) for i in range(2)]
        ybf = [sb(f"ybf{i}", [128, NT], BF16) for i in range(2)]
        y2bf = [sb(f"y2bf{i}", [128, NT], BF16) for i in range(2)]
        msq = sb("msq", [128, NT], F32)
        rstd = sb("rstd", [128, NT], F32)
        mr = sb("mr", [128, NT], F32)
        tmpf = [sb("tmpf0", [128, NT], F32)] * 2
        vecs = sb("vecs_s", [128, 72 + CW * NCH], F32)
        consts_f = xT[:].rearrange("p c t -> p (c t)")[:, 0:384]
        trineg = sb("trineg", [128, 128], BF16)
        onesneg = sb("onesneg", [128, 128], BF16)
        onesmean = sb("onesmean", [128, 128], BF16)
        mask2 = sb("mask2", [128, 2, 128], BF16)
        cTs = sb("cTs", [128, NCH, nseq], F32)
        siluc = sb("siluc", [128, NCH, nseq], F32)
        bada = sb("bada", [128, 72], F32)
        modT = sb("modT", [128, 72, nseq], F32)
        sc1p = sb("sc1p", [128, 3, NCH, nseq], F32)
        gs = sb("gs", [128, 3, NCH, nseq], F32)
        nA = sb("nA", [128, 2, NCH, nseq], F32)
        nB = sb("nB", [128, 2, NCH, nseq], F32)
        wada_s = []
        for tns, nel in ((kT, NCH * S), (vS, NSB * D)):
            flat = tns[:].rearrange("p a b -> p (a b)").bitcast(F32)
            for i in range(min(4, (nel // 2) // 2048)):
                wada_s.append(flat[:, i * 2048:(i + 1) * 2048].rearrange("p (j k n) -> p j k n", j=2, k=NCH))
        NWB = len(wada_s)

        psall = es.enter_context(nc.psum_tensor("psall", [128, 8 * NT], F32))
        banks = [psall[:, i * NT:(i + 1) * NT] for i in range(8)]
        ZP = psall[:, 0:2 * NT].rearrange("p (b n) -> p b n", b=2)
        CP = psall[:, 2 * NT:4 * NT].rearrange("p (b n) -> p b n", b=2)
        work_ctr = [0]

        def wbank():
            i = work_ctr[0] % 4
            work_ctr[0] += 1
            return i

        def mm(bi, lhsT, rhs, start, stop, R, rows=128, cols=(0, NT)):
            T.op(PE, lambda: nc.tensor.matmul(banks[bi][0:rows, cols[0]:cols[1]], lhsT=lhsT, rhs=rhs, start=start, stop=stop),
                 R=R, W=[("bank", bi)])

        cast_step = {}
        wb_keys = {}

        def cast_weight(k, after=()):
            n0 = wshapes[k][0]
            per = int(np.prod(wshapes[k][1:])) * 4
            step = max(1, (6 << 20) // per)
            cast_step[k] = step
            for a in range(0, n0, step):
                b = min(n0, a + step)
                T.dma(POOL, "cast_" + k, wb[k][a:b], wf[k][a:b], R=list(after), W=[("wb", k, a // step)])
            wb_keys[k] = [("wb", k, p) for p in range((n0 + step - 1) // step)]

        slot_ctr = [0]

        slot_owner = [None] * NSLOTS

        def load_slice(k, idx, parts=128, elems=None):
            if k not in cast_step:
                cast_weight(k)
            si = slot_ctr[0] % NSLOTS
            slot_ctr[0] += 1
            slot_owner[si] = None
            n = wshapes[k][2] if elems is None else elems
            T.dma(SP, f"slot{si}", slots[si][0:parts, 0:n], wb[k][idx], R=wb_keys[k], W=[("slot", si)])
            return si

        def chunk_view(k, idx):
            if k not in cast_step:
                cast_weight(k)
            pidx = idx & ~1
            for si in range(NSLOTS):
                if slot_owner[si] == (k, pidx):
                    break
            else:
                si = slot_ctr[0] % NSLOTS
                slot_ctr[0] += 1
                slot_owner[si] = (k, pidx)
                st = cast_step[k]
                T.dma(SP, f"slot{si}", slots[si][:, 0:2048].rearrange("p (a n) -> p a n", a=2),
                      wb[k][pidx:pidx + 2].rearrange("a p n -> p a n"),
                      R=wb_keys[k], W=[("slot", si)])
            a = idx & 1
            return si, slots[si][:, a * 1024:(a + 1) * 1024].rearrange("p (k n) -> p k n", k=NCH)

        XKEYS = [("xT", c) for c in range(NCH)]
        T.dma(SP, "ld_misc", consts_f, consts_d, W=XKEYS)
        T.dma(SP, "ld_misc", vecs[:], vecs_d, W=["vecs"])
        T.dma(SP, "ld_misc", cTs[:], cT_d, W=["cTs"])
        T.dma(SP, "ld_misc", bada[:], bada_d, W=["bada"])
        for kk in XKEYS + ["vecs", "cTs", "bada"]:
            T.last_write[kk] = T.last_write["bada"]

        cast_weight("gu1")
        cast_weight("dn1")
        T.op(DVE, lambda: nc.vector.tensor_copy(out=trineg[:], in_=consts_f[:, 0:128]), R=XKEYS, W=["trineg"])
        for hh in range(2):
            T.op(DVE, lambda hh=hh: nc.vector.tensor_copy(out=mask2[:, hh, :], in_=consts_f[:, 128:256]), R=XKEYS, W=["mask2"])
        T.op(DVE, lambda: nc.vector.tensor_copy(out=ident[:], in_=consts_f[:, 256:384]), R=XKEYS, W=["ident"])
        T.op(DVE, lambda: nc.vector.memset(onesneg[:], -1.0), W=["onesneg"])
        T.op(DVE, lambda: nc.vector.memset(onesmean[:], 1.0 / D), W=["onesmean"])
        T.op(ACT, lambda: nc.scalar.activation(out=siluc[:], in_=cTs[:], func=AF.Silu), R=["cTs"], W=["siluc"])

        CWOFF = 72
        cast_step["cdiag"] = 1
        wb_keys["cdiag"] = [("wb", "cdiag", p) for p in range(2 * NCH)]
        stg = hn[:].rearrange("p c t -> p (c t)")
        for c in range(NCH):
            for half in range(2):
                sidx = (c * 2 + half) % 2
                sv = stg[:, sidx * 2048:(sidx + 1) * 2048].rearrange("p (j n) -> p j n", j=16)
                for j in range(16 - half):
                    tap = half * 16 + j
                    T.op(DVE, lambda sv=sv, j=j, tap=tap, c=c: nc.vector.tensor_scalar(
                        out=sv[:, j, :], in0=ident[:], scalar1=vecs[:, CWOFF + tap * NCH + c:CWOFF + tap * NCH + c + 1],
                        scalar2=None, op0=ALU.mult), R=["ident", "vecs"], W=[("stg", sidx, j)])
                if half == 1:
                    T.op(DVE, lambda sv=sv: nc.vector.memset(sv[:, 15, :], 0.0), W=[("stg", sidx, 15)])
                T.dma(POOL, "st_cdiag%d" % sidx, wb["cdiag"][c * 2 + half], stg[:, sidx * 2048:(sidx + 1) * 2048],
                      R=[("stg", sidx, j) for j in range(16)], W=[("wb", "cdiag", c * 2 + half)])

        for s36 in range(36):
            wbuf = wada_s[s36 % NWB]
            T.dma(SP, f"ld_wada{s36 % NWB}", wbuf, wada_d[s36], W=[("wada", s36 % NWB)])
            for j in range(2):
                m = s36 * 2 + j
                for k in range(NCH):
                    T.op(PE, lambda wbuf=wbuf, j=j, k=k, m=m: nc.tensor.matmul(
                        banks[7][:, m * nseq:(m + 1) * nseq], lhsT=wbuf[:, j, k, :], rhs=siluc[:, k, :],
                        start=(k == 0), stop=(k == NCH - 1)),
                        R=[("wada", s36 % NWB), "siluc"], W=[("bank", 7)])
        for m in range(72):
            T.op(DVE, lambda m=m: nc.vector.tensor_scalar(
                out=modT[:, m, :], in0=banks[7][:, m * nseq:(m + 1) * nseq], scalar1=bada[:, m:m + 1], scalar2=None,
                op0=ALU.add), R=[("bank", 7), "bada"], W=["modT"])

        def modv(i, which):
            a = (3 * i + which) * NCH
            return modT[:, a:a + NCH, :]

        LNG = [0, 16, 32]
        LNB = [8, 24, 40]
        CB, CG, CBETA, CWOFF = 48, 56, 64, 72
        for i in range(3):
            T.op(DVE, lambda i=i: nc.vector.tensor_scalar(out=sc1p[:, i], in0=modv(i, 1), scalar1=1.0, scalar2=None, op0=ALU.add),
                 R=["modT"], W=["sc1p"])
            gmul = (0.5 / ALPHA) if i != 1 else (1.0 / ALPHA)
            T.op(DVE, lambda i=i, gmul=gmul: nc.vector.tensor_scalar(out=gs[:, i], in0=modv(i, 2), scalar1=gmul, scalar2=None, op0=ALU.mult),
                 R=["modT"], W=["gs"])
        for i in (1, 2):
            for b in range(nseq):
                T.op(DVE, lambda i=i, b=b: nc.vector.tensor_tensor(
                    out=nA[:, i - 1, :, b], in0=sc1p[:, i, :, b], in1=vecs[:, LNG[i - 1]:LNG[i - 1] + NCH], op=ALU.mult),
                    R=["sc1p", "vecs"], W=["nA"])
                T.op(DVE, lambda i=i, b=b: nc.vector.tensor_tensor(
                    out=nB[:, i - 1, :, b], in0=sc1p[:, i, :, b], in1=vecs[:, LNB[i - 1]:LNB[i - 1] + NCH], op=ALU.mult),
                    R=["sc1p", "vecs"], W=["nB"])
                T.op(DVE, lambda i=i, b=b: nc.vector.tensor_tensor(
                    out=nB[:, i - 1, :, b], in0=nB[:, i - 1, :, b], in1=modT[:, (3 * i) * NCH:(3 * i + 1) * NCH, b], op=ALU.add),
                    R=["nB", "modT"], W=["nB"])
        SETUP_KEYS = ["sc1p", "gs", "nA", "nB", "vecs", "modT"]
        tmp_ctr = [0]

        def epilogue(b, sub, produce, final):
            pend = None
            for c in range(NCH):
                bi = produce(c)
                if pend is not None and not DBG_A:
                    stats_mm(*pend)
                T.op(DVE, lambda c=c, bi=bi: nc.vector.scalar_tensor_tensor(
                    out=xT[:, c, :], in0=banks[bi][:], scalar=gs[:, sub, c, b:b + 1], in1=xT[:, c, :],
                    op0=ALU.mult, op1=ALU.add), R=[("bank", bi), ("xT", c)] + SETUP_KEYS, W=[("xT", c)])
                ti = tmp_ctr[0] % 2
                tmp_ctr[0] += 1
                T.op(ACT, lambda c=c, ti=ti: nc.scalar.copy(out=ybf[ti][:], in_=xT[:, c, :]), R=[("xT", c)], W=[("ybf", ti)])
                T.op(ACT, lambda c=c, ti=ti: nc.scalar.activation(out=y2bf[ti][:], in_=xT[:, c, :], func=AF.Square),
                     R=[("xT", c)], W=[("y2bf", ti)])
                pend = (c, ti)
                if DBG_A:
                    stats_mm(*pend)
            if not DBG_A:
                stats_mm(*pend)
            ln_finalize(EPS_RES)
            for c in range(NCH):
                normalize(xT[:, c, :], ("xT", c))
                if not final:
                    T.op(ACT, lambda c=c: nc.scalar.activation(
                        out=uT[:, c, :], in_=xT[:, c, :], func=AF.Identity,
                        scale=nA[:, sub, c, b:b + 1], bias=nB[:, sub, c, b:b + 1]),
                        R=[("xT", c)] + SETUP_KEYS, W=[("uT", c)])
                    T.op(ACT, lambda c=c: nc.scalar.activation(
                        out=xT[:, c, :], in_=xT[:, c, :], func=AF.Identity,
                        scale=vecs[:, LNG[sub] + c:LNG[sub] + c + 1], bias=vecs[:, LNB[sub] + c:LNB[sub] + c + 1]),
                        R=[("xT", c)] + SETUP_KEYS, W=[("xT", c)])
                else:
                    T.op(ACT, lambda c=c: nc.scalar.activation(
                        out=oT[:, c, :], in_=xT[:, c, :], func=AF.Identity,
                        scale=vecs[:, LNG[sub] + c:LNG[sub] + c + 1], bias=vecs[:, LNB[sub] + c:LNB[sub] + c + 1]),
                        R=[("xT", c)] + SETUP_KEYS, W=[("oT", c)])

        def stats_mm(c, ti):
            mm(4, onesmean[:], ybf[ti][:], c == 0, c == NCH - 1, R=["onesmean", ("ybf", ti)])
            mm(5, onesmean[:], y2bf[ti][:], c == 0, c == NCH - 1, R=["onesmean", ("y2bf", ti)])

        def ln_finalize(eps):
            T.op(ACT, lambda: nc.scalar.activation(out=msq[:], in_=banks[4][:], func=AF.Square), R=[("bank", 4)], W=["msq"])
            T.op(DVE, lambda: nc.vector.tensor_tensor(out=rstd[:], in0=banks[5][:], in1=msq[:], op=ALU.subtract),
                 R=[("bank", 5), "msq"], W=["rstd"])
            T.op(DVE, lambda: nc.vector.tensor_scalar(out=rstd[:], in0=rstd[:], scalar1=eps, scalar2=None, op0=ALU.add),
                 R=["rstd"], W=["rstd"])
            T.op(ACT, lambda: nc.scalar.activation(out=rstd[:], in_=rstd[:], func=AF.Ln), R=["rstd"], W=["rstd"])
            T.op(ACT, lambda: nc.scalar.activation(out=rstd[:], in_=rstd[:], func=AF.Exp, scale=-0.5), R=["rstd"], W=["rstd"])
            T.op(DVE, lambda: nc.vector.tensor_tensor(out=mr[:], in0=banks[4][:], in1=rstd[:], op=ALU.mult),
                 R=[("bank", 4), "rstd"], W=["mr"])

        def normalize(ap, key):
            T.op(DVE, lambda: nc.vector.tensor_tensor(out=ap, in0=ap, in1=rstd[:], op=ALU.mult), R=[key, "rstd"], W=[key])
            T.op(DVE, lambda: nc.vector.tensor_tensor(out=ap, in0=ap, in1=mr[:], op=ALU.subtract), R=[key, "mr"], W=[key])

        def ffn(b, sub, kgu, kdn, final):
            pre = {}
            sis = [load_slice(kgu, f) for f in range(2)]
            wvs = [slots[si][:, 0:NCH * 256].rearrange("p (k n) -> p k n", k=NCH) for si in sis]
            pbk = [wbank() for _ in range(4)]
            for k in range(NCH):
                for f in range(2):
                    mm(pbk[2 * f], wvs[f][:, k, 0:128], uT[:, k, :], k == 0, k == NCH - 1, R=[("slot", sis[f]), ("uT", k)])
                    mm(pbk[2 * f + 1], wvs[f][:, k, 128:256], uT[:, k, :], k == 0, k == NCH - 1, R=[("slot", sis[f]), ("uT", k)])
            for f in range(2):
                pre[f] = (pbk[2 * f], pbk[2 * f + 1])
            for f in range(NF):
                if f in pre:
                    ba, bg = pre[f]
                else:
                    si = load_slice(kgu, f)
                    wv_ = slots[si][:, 0:NCH * 256].rearrange("p (k n) -> p k n", k=NCH)
                    ba = wbank()
                    for k in range(NCH):
                        mm(ba, wv_[:, k, 0:128], uT[:, k, :], k == 0, k == NCH - 1, R=[("slot", si), ("uT", k)])
                    bg = wbank()
                    for k in range(NCH):
                        mm(bg, wv_[:, k, 128:256], uT[:, k, :], k == 0, k == NCH - 1, R=[("slot", si), ("uT", k)])
                ti = tmp_ctr[0] % 2
                tmp_ctr[0] += 1
                T.op(ACT, lambda ba=ba, ti=ti: nc.scalar.activation(out=sa_t[ti][:], in_=banks[ba][:], func=AF.Silu),
                     R=[("bank", ba)], W=[("sa", ti)])
                T.op(DVE, lambda bg=bg, ti=ti, f=f: nc.vector.tensor_tensor(out=hT[:, f, :], in0=banks[bg][:], in1=sa_t[ti][:], op=ALU.mult),
                     R=[("bank", bg), ("sa", ti)], W=[("hT", f)])

            def produce(c):
                bi = wbank()
                for half in range(2):
                    si = load_slice(kdn, c * 2 + half)
                    wd_ = slots[si][:, 0:11 * 128].rearrange("p (f n) -> p f n", f=11)
                    for f2 in range(11):
                        f = half * 11 + f2
                        mm(bi, wd_[:, f2, :], hT[:, f, :], f == 0, f == NF - 1, R=[("slot", si), ("hT", f)])
                return bi
            epilogue(b, sub, produce, final)

        def proj_chunk(j):
            si, w_ = chunk_view("win", j)
            bi = wbank()
            for k in range(NCH):
                mm(bi, w_[:, k, :], uT[:, k, :], k == 0, k == NCH - 1, R=[("slot", si), ("uT", k)])
            return bi

        def mixer(b, g):
            t0 = g * NT
            cvs = [chunk_view("win", j) for j in range(4)]
            qbk = [wbank() for _ in range(4)]
            for k in range(NCH):
                for j in range(4):
                    mm(qbk[j], cvs[j][1][:, k, :], uT[:, k, :], k == 0, k == NCH - 1, R=[("slot", cvs[j][0]), ("uT", k)])
            for j in range(NCH):
                bi = qbk[j] if j < 4 else proj_chunk(j)
                T.op(ACT, lambda bi=bi, j=j: nc.scalar.mul(out=qT[:, j, :], in_=banks[bi][:], mul=0.125),
                     R=[("bank", bi)], W=[("qT", j)])
            for j in range(NCH):
                bi = proj_chunk(NCH + j)
                T.op(DVE, lambda bi=bi, j=j: nc.vector.tensor_copy(out=kT[:, j, t0:t0 + NT], in_=banks[bi][:]),
                     R=[("bank", bi)], W=[("kT", j, g)])
            for q4 in range(4):
                si = load_slice("wv", q4)
                w_ = slots[si][:, 0:NCH * 256].rearrange("p (k n) -> p k n", k=NCH)
                for tt in range(4):
                    bi = wbank()
                    for k in range(NCH):
                        T.op(PE, lambda bi=bi, k=k, tt=tt, w_=w_: nc.tensor.matmul(
                            banks[bi][:, 0:256], lhsT=uT[:, k, tt * 128:(tt + 1) * 128], rhs=w_[:, k, :],
                            start=(k == 0), stop=(k == NCH - 1)), R=[("slot", si), ("uT", k)], W=[("bank", bi)])
                    sbk = g * 4 + tt
                    eng = ACT if (tt % 2 == 0) else DVE
                    if eng is ACT:
                        T.op(ACT, lambda bi=bi, sbk=sbk, q4=q4: nc.scalar.copy(out=vS[:, sbk, q4 * 256:(q4 + 1) * 256], in_=banks[bi][:, 0:256]),
                             R=[("bank", bi)], W=[("vS", sbk, q4)])
                    else:
                        T.op(DVE, lambda bi=bi, sbk=sbk, q4=q4: nc.vector.tensor_copy(out=vS[:, sbk, q4 * 256:(q4 + 1) * 256], in_=banks[bi][:, 0:256]),
                             R=[("bank", bi)], W=[("vS", sbk, q4)])
            if b == 0 and g == 0:
                for kk in ("wsb", "wco", "wout", "gu2", "dn2"):
                    cast_weight(kk)
            nblk = 4 * g + 4
            units = [(c, n, i) for c in range(NCH) for n, i in enumerate(range(nblk - 1, -1, -1))]

            def cols_of(i):
                m = i - 4 * g
                return (128 * m if m > 0 else 0), m

            def s1_pe(u, c, n, i):
                lo, m = cols_of(i)
                ss = c % 2
                if n == 0:
                    T.op(POOL, lambda ss=ss: nc.gpsimd.memset(spsum[ss][:], 0.0), W=[("spsum", ss)])
                for hh in range(2):
                    po = hh * 64
                    mm(hh, kT[po:po + 64, c, i * 128:(i + 1) * 128], qT[po:po + 64, c, lo:NT], True, True,
                       R=[("kT", c, i // 4), ("qT", c)], cols=(lo, NT))

            def s1_act(u, c, n, i):
                lo, m = cols_of(i)
                ei, si = 0, u % 3
                T.op(ACT, lambda: nc.scalar.activation(out=e_t[ei][:, :, lo:NT], in_=ZP[:, :, lo:NT], func=AF.Exp),
                     R=[("bank", 0), ("bank", 1)], W=[("e", ei)])
                T.op(ACT, lambda: nc.scalar.activation(out=sp_t[si][:, :, lo:NT], in_=e_t[ei][:, :, lo:NT], func=AF.Ln, bias=1.0),
                     R=[("e", ei)], W=[("sp", si)])
                if m >= 0:
                    T.op(DVE, lambda: nc.vector.tensor_tensor(out=sp_t[si][:, :, lo:lo + 128], in0=sp_t[si][:, :, lo:lo + 128],
                                                              in1=mask2[:], op=ALU.mult), R=[("sp", si), "mask2"], W=[("sp", si)])

            def s2_pe(u, c, n, i):
                lo, m = cols_of(i)
                si = u % 3
                ss = c % 2
                for hh in range(2):
                    po = hh * 64
                    mm(2 + hh, kT[po:po + 64, c, i * 128:(i + 1) * 128], qT[po:po + 64, c, lo:NT], True, False,
                       R=[("kT", c, i // 4), ("qT", c)], cols=(lo, NT))
                    mm(2 + hh, trineg[:], sp_t[si][:, hh, lo:NT], False, n == 0, R=["trineg", ("sp", si)], cols=(lo, NT))
                    if n > 0:
                        mm(2 + hh, onesneg[:], spsum[ss][:, hh, lo:NT], False, True, R=["onesneg", ("spsum", ss)], cols=(lo, NT))
                if i > 0:
                    T.op(DVE, lambda: nc.vector.tensor_tensor(out=spsum[ss][:, :, lo:NT], in0=spsum[ss][:, :, lo:NT],
                                                              in1=sp_t[si][:, :, lo:NT], op=ALU.add),
                         R=[("spsum", ss), ("sp", si)], W=[("spsum", ss)])

            def s2_act(u, c, n, i):
                lo, m = cols_of(i)
                wi = u % 2
                T.op(ACT, lambda: nc.scalar.activation(out=w_t[wi][:, :, lo:NT], in_=CP[:, :, lo:NT], func=AF.Exp),
                     R=[("bank", 2), ("bank", 3)], W=[("w", wi)])
                if m >= 0:
                    T.op(DVE, lambda: nc.vector.tensor_tensor(out=w_t[wi][:, :, lo:lo + 128], in0=w_t[wi][:, :, lo:lo + 128],
                                                              in1=mask2[:], op=ALU.mult), R=[("w", wi), "mask2"], W=[("w", wi)])

            def s3_pe(u, c, n, i):
                lo, m = cols_of(i)
                wi = u % 2
                for hh in range(2):
                    T.op(PE, lambda hh=hh: nc.tensor.matmul(
                        banks[6 + hh][:, lo:NT], lhsT=vS[:, i, c * 128:(c + 1) * 128], rhs=w_t[wi][:, hh, lo:NT],
                        start=(n == 0), stop=(n == nblk - 1), skip_group_check=True),
                        R=[("vS", i, c // 2), ("w", wi)], W=[("bank", 6 + hh)])
                if n == nblk - 1:
                    T.op(DVE, lambda: nc.vector.tensor_copy(out=ysb[0:64, c, :], in_=banks[6][0:64, :]),
                         R=[("bank", 6)], W=[("ysb", c)])
                    T.op(DVE, lambda: nc.vector.tensor_copy(out=ysb[64:128, c, :], in_=banks[7][64:128, :]),
                         R=[("bank", 7)], W=[("ysb", c)])

            def fillers():
                for _ in range(N_WARM):
                    T.op(PE, lambda: nc.tensor.matmul(banks[4][:], lhsT=trineg[:], rhs=ybf[0][:], start=True, stop=True),
                         R=["trineg", ("ybf", 0)], W=[("bank", 4)])

            U = len(units)
            for u0 in range(3):
                s1_pe(u0, *units[u0])
                s1_act(u0, *units[u0])
            s2_pe(0, *units[0])
            s2_act(0, *units[0])
            for u in range(U):
                if u + 1 < U:
                    s2_pe(u + 1, *units[u + 1])
                s3_pe(u, *units[u])
                if u + 3 < U:
                    s1_pe(u + 3, *units[u + 3])
                fillers()
                if u + 1 < U:
                    s2_act(u + 1, *units[u + 1])
                if u + 3 < U:
                    s1_act(u + 3, *units[u + 3])
            def conv_P(c):
                b1 = proj_chunk(24 + c)
                b2 = proj_chunk(32 + c)
                ti = tmp_ctr[0] % 2
                tmp_ctr[0] += 1
                hb = hbuf[c % 2]
                T.op(ACT, lambda: nc.scalar.activation(out=sa_t[ti][:], in_=banks[b2][:], func=AF.Sigmoid),
                     R=[("bank", b2)], W=[("sa", ti)])
                if g == 0:
                    T.op(POOL, lambda: nc.gpsimd.memset(hb[:, 0:30], 0.0), W=[("hbuf", c % 2)])
                else:
                    T.op(POOL, lambda: nc.gpsimd.tensor_copy(out=hb[:, 0:30], in_=halo[:, c, :]),
                         R=[("halo", c)], W=[("hbuf", c % 2)])
                T.op(DVE, lambda: nc.vector.tensor_tensor(out=hb[:, 30:30 + NT], in0=banks[b1][:], in1=sa_t[ti][:], op=ALU.mult),
                     R=[("bank", b1), ("sa", ti)], W=[("hbuf", c % 2)])
                T.op(POOL, lambda: nc.gpsimd.tensor_copy(out=halo[:, c, :], in_=hb[:, NT:NT + 30]),
                     R=[("hbuf", c % 2)], W=[("halo", c)])

            def conv_C(c):
                hb = hbuf[c % 2]
                ti = c % 2
                bcv = wbank()
                for half in range(2):
                    si = load_slice("cdiag", c * 2 + half)
                    dsl = slots[si][:, 0:2048].rearrange("p (j n) -> p j n", j=16)
                    for j in range(16 - half):
                        tap = half * 16 + j
                        mm(bcv, dsl[:, j, :], hb[:, tap:tap + NT], tap == 0, tap == CW - 1, R=[("slot", si), ("hbuf", c % 2)])
                T.op(DVE, lambda: nc.vector.tensor_scalar(
                    out=cv[:, c, :], in0=banks[bcv][:], scalar1=vecs[:, CB + c:CB + c + 1], scalar2=None, op0=ALU.add),
                    R=[("bank", bcv), "vecs"], W=[("cv", c)])
                T.op(ACT, lambda: nc.scalar.copy(out=ybf[ti][:], in_=cv[:, c, :]), R=[("cv", c)], W=[("ybf", ti)])
                T.op(ACT, lambda: nc.scalar.activation(out=y2bf[ti][:], in_=cv[:, c, :], func=AF.Square),
                     R=[("cv", c)], W=[("y2bf", ti)])
                return (c, ti)

            conv_P(0)
            pend = None
            for c in range(NCH):
                if c + 1 < NCH:
                    conv_P(c + 1)
                nxt = conv_C(c)
                if pend is not None:
                    stats_mm(*pend)
                pend = nxt
            stats_mm(*pend)
            ln_finalize(LN_EPS)
            for c in range(NCH):
                si, w_ = chunk_view("wsb", c)
                b1 = wbank()
                for k in range(NCH):
                    mm(b1, w_[:, k, :], ysb[:, k, :], k == 0, k == NCH - 1, R=[("slot", si), ("ysb", k)])
                b2 = proj_chunk(40 + c)
                ti = tmp_ctr[0] % 2
                tmp_ctr[0] += 1
                T.op(ACT, lambda b2=b2, ti=ti: nc.scalar.activation(out=sa_t[ti][:], in_=banks[b2][:], func=AF.Sigmoid),
                     R=[("bank", b2)], W=[("sa", ti)])
                normalize(cv[:, c, :], ("cv", c))
                T.op(DVE, lambda b1=b1, ti=ti, c=c: nc.vector.tensor_tensor(out=mrg[:, c, :], in0=banks[b1][:], in1=sa_t[ti][:], op=ALU.mult),
                     R=[("bank", b1), ("sa", ti)], W=[("mrg", c)])
            for c in range(NCH):
                T.op(ACT, lambda c=c: nc.scalar.activation(
                    out=hn[:, c, :], in_=cv[:, c, :], func=AF.Silu,
                    scale=vecs[:, CG + c:CG + c + 1], bias=vecs[:, CBETA + c:CBETA + c + 1]),
                    R=[("cv", c), "vecs"], W=[("hn", c)])
            for c in range(NCH):
                si, w_ = chunk_view("wco", c)
                b1 = wbank()
                for k in range(NCH):
                    mm(b1, w_[:, k, :], hn[:, k, :], k == 0, k == NCH - 1, R=[("slot", si), ("hn", k)])
                b2 = proj_chunk(48 + c)
                ti = tmp_ctr[0] % 2
                tmp_ctr[0] += 1
                T.op(ACT, lambda b2=b2, ti=ti: nc.scalar.activation(out=sa_t[ti][:], in_=banks[b2][:], func=AF.Sigmoid),
                     R=[("bank", b2)], W=[("sa", ti)])
                T.op(DVE, lambda b1=b1, ti=ti: nc.vector.tensor_tensor(out=tmpf[ti][:], in0=banks[b1][:], in1=sa_t[ti][:], op=ALU.mult),
                     R=[("bank", b1), ("sa", ti)], W=[("tmpf", 0)])
                T.op(DVE, lambda ti=ti, c=c: nc.vector.tensor_tensor(out=qT[:, c, :], in0=tmpf[ti][:], in1=mrg[:, c, :], op=ALU.add),
                     R=[("tmpf", 0), ("mrg", c)], W=[("qT", c)])

            def produce(c):
                si, w_ = chunk_view("wout", c)
                bi = wbank()
                for k in range(NCH):
                    mm(bi, w_[:, k, :], qT[:, k, :], k == 0, k == NCH - 1, R=[("slot", si), ("qT", k)])
                return bi
            epilogue(b, 1, produce, False)

        for b in range(nseq):
            for g in range(NG):
                t0 = g * NT
                T.dma(POOL, "ld_x", xT[:], xT_d[b].rearrange("(c p) t -> p c t", p=128)[:, :, t0:t0 + NT],
                      W=[("xT", c) for c in range(NCH)])
                if b == 0 and g == 0:
                    cast_weight("win")
                for c in range(NCH):
                    T.op(DVE, lambda c=c: nc.vector.tensor_scalar(
                        out=uT[:, c, :], in0=xT[:, c, :], scalar1=sc1p[:, 0, c, b:b + 1], scalar2=modT[:, c, b:b + 1],
                        op0=ALU.mult, op1=ALU.add), R=[("xT", c)] + SETUP_KEYS, W=[("uT", c)])
                ffn(b, 0, "gu1", "dn1", False)
                mixer(b, g)
                ffn(b, 2, "gu2", "dn2", True)
                T.dma(POOL, "out_x", outT_d[b].rearrange("(c p) t -> p c t", p=128)[:, :, t0:t0 + NT], oT,
                      R=[("oT", c) for c in range(NCH)] + [("hT", f) for f in range(NF)])
        T.final_wait(POOL)
    return nc, T.recorded


def _tile_cols(W, width=128):
    K, N = W.shape
    t = W.reshape(K // 128, 128, N // width, width).transpose(2, 1, 0, 3)
    return np.ascontiguousarray(t).reshape(N // width, 128, (K // 128) * width)


def _prep_shared(inp):
    f = lambda a: np.asarray(a, dtype=np.float32)
    sh = {}
    w_ada = f(inp["w_ada"])[0]
    t = w_ada.reshape(NCH, 128, 36, 2, 128).transpose(2, 1, 3, 0, 4)
    sh["w_ada_t"] = np.ascontiguousarray(t)
    sh["b_ada_t"] = np.ascontiguousarray(f(inp["b_ada"])[0].reshape(72, 128).T)

    def fm(v):
        return f(v).reshape(NCH, 128).T

    cols = [fm(inp[k][0]) for k in ("ln1_g", "ln1_b", "ln2_g", "ln2_b", "ln3_g", "ln3_b", "conv_b", "conv_ln_g", "conv_ln_b")]
    cw = f(inp["conv_w"])[0]
    cwt = cw.reshape(CW, NCH, 128).transpose(2, 0, 1).reshape(128, CW * NCH)
    sh["vecs"] = np.ascontiguousarray(np.concatenate(cols + [cwt], axis=1))
    j = np.arange(128)[:, None]
    s = np.arange(128)[None, :]
    trineg = np.where(j >= s, -1.0, 0.0).astype(np.float32)
    mask = np.where(j < s, 1.0, 0.0).astype(np.float32)
    sh["consts"] = np.ascontiguousarray(np.concatenate([trineg, mask, np.eye(128, dtype=np.float32)], axis=1))
    for nm, key in (("gu1", "ffn1_w_gu"), ("gu2", "ffn2_w_gu")):
        W = f(inp[key])[0]
        a = _tile_cols(W[:, :DFF]).reshape(NF, 128, NCH, 128)
        gg = _tile_cols(W[:, DFF:]).reshape(NF, 128, NCH, 128)
        sh["w_" + nm] = np.ascontiguousarray(np.concatenate([a, gg], axis=3)).reshape(NF, 128, NCH * 256)
    for nm, key in (("dn1", "ffn1_w_down"), ("dn2", "ffn2_w_down")):
        t = _tile_cols(f(inp[key])[0]).reshape(NCH, 128, 2, 11 * 128).transpose(0, 2, 1, 3)
        sh["w_" + nm] = np.ascontiguousarray(t).reshape(2 * NCH, 128, 11 * 128)
    w_in = f(inp["w_in"])[0]
    sh["w_win"] = _tile_cols(w_in)
    sh["w_wv"] = _tile_cols(w_in[:, 2048:3072], 256)
    sh["w_wsb"] = _tile_cols(f(inp["w_sb_out"])[0])
    sh["w_wco"] = _tile_cols(f(inp["w_conv_out"])[0])
    sh["w_wout"] = _tile_cols(f(inp["w_out"])[0])
    return sh


def _run(inp, ncores, nseq, S, trace=False):
    sh = _prep_shared(inp)
    x = np.asarray(inp["x"], dtype=np.float32)
    c = np.asarray(inp["c"], dtype=np.float32)
    in_maps = []
    for i in range(ncores):
        xs = x[i * nseq:(i + 1) * nseq]
        cs = c[i * nseq:(i + 1) * nseq]
        m = dict(sh)
        m["xT"] = np.ascontiguousarray(xs.transpose(0, 2, 1))
        m["cT"] = np.ascontiguousarray(cs.reshape(nseq, NCH, 128).transpose(2, 1, 0))
        in_maps.append(m)
    nc = build_nc(nseq, S)
    res = run_bass_kernel_spmd(nc, in_maps, core_ids=list(range(ncores)), **({"trace": True} if trace else {}))
    outs = [np.asarray(r["outT"]).transpose(0, 2, 1) for r in res.results]
    return np.ascontiguousarray(np.concatenate(outs, axis=0)).astype(np.float32), res


def kernel(**inputs):
    B = inputs["x"].shape[0]
    S = inputs["x"].shape[1]
    out, _ = _run(inputs, NCORES, B // NCORES, S)
    return out
```
